# Optimizing a Trainium2 kernel written in Bass

```python
import math
import jax
import jax.numpy as jnp
from jax import lax
import numpy as np

D_MODEL = 1024
BATCH = 8
SEQ = 2048
DEPTH = 4

CTX_LEN = 256
GRID_W = 64
N_BRANCH = 3
N_MOD = 9
D_FF = 2816
MLA_HEADS = 8
MLA_NOPE = 64
MLA_ROPE = 32
MLA_QK = MLA_NOPE + MLA_ROPE
MLA_V = 64
MLA_Q_RANK = 384
MLA_KV_RANK = 256
DIFF_HEADS = 4
DIFF_DIM = 64
DIFF_V = 2 * DIFF_DIM
CONV_WIDTH = 512
CONV_K = 3
BRANCH_WIDTH = 512
Q_BLOCK = 128
ROPE_THETA = 10000.0
EPS = 1e-6

IN_SIZES = (MLA_Q_RANK, MLA_KV_RANK, MLA_ROPE,
            DIFF_HEADS * 2 * DIFF_DIM, DIFF_HEADS * 2 * DIFF_DIM, DIFF_HEADS * DIFF_V,
            CONV_WIDTH, CONV_WIDTH, CONV_WIDTH,
            N_BRANCH * D_MODEL)
IN_WIDTH = sum(IN_SIZES)
IN_OFFSETS = tuple(sum(IN_SIZES[:i + 1]) for i in range(len(IN_SIZES) - 1))

kernel_name = 'hybrid_mla_diffattn_shortconv_macaron_dit'


def rms_norm(x, g):
    xf = x.astype(jnp.float32)
    y = xf * lax.rsqrt(jnp.mean(xf * xf, axis=-1, keepdims=True) + EPS)
    return (y * g.astype(jnp.float32)).astype(x.dtype)


def modulate(x, g, shift, scale):
    return rms_norm(x, g) * (1.0 + scale) + shift


def swiglu(h, w_up, w_down):
    a, b = jnp.split(h @ w_up, 2, axis=-1)
    return (jax.nn.silu(a) * b) @ w_down


def axial_angles(n_tokens, rot_dim):
    rows = n_tokens // GRID_W
    row = jnp.broadcast_to(jnp.arange(rows, dtype=jnp.float32)[:, None], (rows, GRID_W)).reshape(-1)
    col = jnp.broadcast_to(jnp.arange(GRID_W, dtype=jnp.float32)[None, :], (rows, GRID_W)).reshape(-1)
    quarter = rot_dim // 4
    inv_freq = ROPE_THETA ** (-jnp.arange(quarter, dtype=jnp.float32) / quarter)
    return jnp.concatenate([row[:, None] * inv_freq, col[:, None] * inv_freq], axis=-1)


def apply_rope2d(x, ang):
    r = x.shape[-1]
    q4 = r // 4
    xs = x.astype(jnp.float32).reshape(x.shape[:-1] + (2, 2, q4))
    extra = x.ndim - 3
    a = ang.reshape((ang.shape[0],) + (1,) * extra + (2, q4))
    cos, sin = jnp.cos(a), jnp.sin(a)
    x1, x2 = xs[..., 0, :], xs[..., 1, :]
    out = jnp.stack([x1 * cos - x2 * sin, x2 * cos + x1 * sin], axis=-2)
    return out.reshape(x.shape).astype(x.dtype)


def rope_tail(x, ang, rot_dim):
    return jnp.concatenate([x[..., :-rot_dim], apply_rope2d(x[..., -rot_dim:], ang)], axis=-1)


def attend(q, k, v, mix, scale):
    b, sq, h, m, dk = q.shape
    blk = min(Q_BLOCK, sq)
    nb = sq // blk
    qb = jnp.moveaxis(q.reshape(b, nb, blk, h, m, dk), 1, 0)

    def one_block(qi):
        s = jnp.einsum('bqhmd,bkhmd->bhmqk', qi, k, preferred_element_type=jnp.float32) * scale
        p = jnp.einsum('bhmqk,m->bhqk', jax.nn.softmax(s, axis=-1), mix)
        return jnp.einsum('bhqk,bkhe->bqhe', p.astype(v.dtype), v)

    o = lax.map(one_block, qb)
    return jnp.moveaxis(o, 0, 1).reshape(b, sq, h, v.shape[-1])


def short_conv(u, w):
    t = u.shape[1]
    half = CONV_K // 2
    up = jnp.pad(u, ((0, 0), (half, half), (0, 0)))
    y = w[0] * up[:, :t]
    for j in range(1, CONV_K):
        y = y + w[j] * up[:, j:j + t]
    return y


def merge(branches, gate_logits, w_br, w_o):
    b, s, _ = gate_logits.shape
    y = jnp.stack(branches, axis=2)
    g = jax.nn.sigmoid(gate_logits.reshape(b, s, N_BRANCH, D_MODEL))
    m = jnp.sum(g * jnp.einsum('bsnw,nwd->bsnd', y, w_br), axis=2)
    return m @ w_o


def mixer(h_ctx, h_lat, ang_mla, ang_diff, p, lam_init, need_ctx):
    b, s, _ = h_lat.shape
    n_ctx = h_ctx.shape[1]
    t = n_ctx + s
    h_all = jnp.concatenate([h_ctx, h_lat], axis=1)
    proj = h_all @ p['w_in']
    (c_q, c_kv, k_rope, dq, dk, dv, conv_b, conv_c, conv_x, gate_logits) = jnp.split(proj, IN_OFFSETS, axis=-1)

    q = (rms_norm(c_q, p['g_cq']) @ p['w_uq']).reshape(b, t, MLA_HEADS, MLA_QK)
    kv = (rms_norm(c_kv, p['g_ckv']) @ p['w_ukv']).reshape(b, t, MLA_HEADS, MLA_NOPE + MLA_V)
    k_nope, v = kv[..., :MLA_NOPE], kv[..., MLA_NOPE:]
    k = jnp.concatenate([k_nope, jnp.broadcast_to(k_rope[:, :, None, :], (b, t, MLA_HEADS, MLA_ROPE))], axis=-1)
    q = rms_norm(q, p['g_q_mla'])
    k = rms_norm(k, p['g_k_mla'])
    k = jnp.concatenate([k[:, :n_ctx], rope_tail(k[:, n_ctx:], ang_mla, MLA_ROPE)], axis=1)
    q_lat = rope_tail(q[:, n_ctx:], ang_mla, MLA_ROPE)
    one = jnp.ones((1,), jnp.float32)
    mla_scale = MLA_QK ** -0.5
    mla_lat = attend(q_lat[:, :, :, None], k[:, :, :, None], v, one, mla_scale).reshape(b, s, BRANCH_WIDTH)

    dq = rms_norm(dq.reshape(b, t, DIFF_HEADS, 2, DIFF_DIM), p['g_q_diff'])
    dk = rms_norm(dk.reshape(b, t, DIFF_HEADS, 2, DIFF_DIM), p['g_k_diff'])
    dv = dv.reshape(b, t, DIFF_HEADS, DIFF_V)
    dk = jnp.concatenate([dk[:, :n_ctx], apply_rope2d(dk[:, n_ctx:], ang_diff)], axis=1)
    dq_lat = apply_rope2d(dq[:, n_ctx:], ang_diff)
    lv = p['lam'].astype(jnp.float32)
    lam = jnp.exp(jnp.sum(lv[0] * lv[1])) - jnp.exp(jnp.sum(lv[2] * lv[3])) + lam_init
    diff_mix = jnp.stack([jnp.ones_like(lam), -lam])
    diff_scale = DIFF_DIM ** -0.5

    def diff_out(o):
        return (rms_norm(o, p['g_subln']) * (1.0 - lam_init)).reshape(o.shape[0], o.shape[1], BRANCH_WIDTH)

    diff_lat = diff_out(attend(dq_lat, dk, dv, diff_mix, diff_scale))

    u = conv_c * conv_x
    conv_lat = conv_b[:, n_ctx:] * short_conv(u[:, n_ctx:], p['conv_w'])

    out_lat = merge([mla_lat, diff_lat, conv_lat], gate_logits[:, n_ctx:], p['w_br'], p['w_o'])
    if not need_ctx:
        return None, out_lat

    mla_ctx = attend(q[:, :n_ctx, :, None], k[:, :n_ctx, :, None], v[:, :n_ctx], one, mla_scale).reshape(b, n_ctx, BRANCH_WIDTH)
    diff_ctx = diff_out(attend(dq[:, :n_ctx], dk[:, :n_ctx], dv[:, :n_ctx], diff_mix, diff_scale))
    conv_ctx = conv_b[:, :n_ctx] * short_conv(u[:, :n_ctx], p['conv_w'])
    out_ctx = merge([mla_ctx, diff_ctx, conv_ctx], gate_logits[:, :n_ctx], p['w_br'], p['w_o'])
    return out_ctx, out_lat


def setup_inputs(seed: int = 0) -> dict:
    key = jax.random.key(seed)
    ks = iter(jax.random.split(key, 32))

    def normal(shape, scale):
        return scale * jax.random.normal(next(ks), shape, jnp.float32)

    def gain(shape):
        return 1.0 + 0.02 * jax.random.normal(next(ks), shape, jnp.float32)

    d, n = D_MODEL, DEPTH
    return {
        'x': normal((BATCH, SEQ, d), 1.0),
        'c': normal((BATCH, d), 1.0),
        'ctx': normal((BATCH, CTX_LEN, d), 1.0),
        'c_ctx': normal((d,), 1.0),
        'w_mod': normal((n, d, N_MOD * d), 0.5 * d ** -0.5),
        'b_mod': normal((n, N_MOD * d), 0.01),
        'norm_g': gain((n, 3, d)),
        'ffn1_up': normal((n, d, 2 * D_FF), d ** -0.5),
        'ffn1_down': normal((n, D_FF, d), D_FF ** -0.5),
        'ffn2_up': normal((n, d, 2 * D_FF), d ** -0.5),
        'ffn2_down': normal((n, D_FF, d), D_FF ** -0.5),
        'w_in': normal((n, d, IN_WIDTH), d ** -0.5),
        'g_cq': gain((n, MLA_Q_RANK)),
        'w_uq': normal((n, MLA_Q_RANK, MLA_HEADS * MLA_QK), MLA_Q_RANK ** -0.5),
        'g_ckv': gain((n, MLA_KV_RANK)),
        'w_ukv': normal((n, MLA_KV_RANK, MLA_HEADS * (MLA_NOPE + MLA_V)), MLA_KV_RANK ** -0.5),
        'g_q_mla': gain((n, MLA_QK)),
        'g_k_mla': gain((n, MLA_QK)),
        'g_q_diff': gain((n, DIFF_DIM)),
        'g_k_diff': gain((n, DIFF_DIM)),
        'lam': normal((n, 4, DIFF_DIM), 0.1),
        'g_subln': gain((n, DIFF_V)),
        'conv_w': normal((n, CONV_K, CONV_WIDTH), CONV_K ** -0.5),
        'w_br': normal((n, N_BRANCH, BRANCH_WIDTH, d), BRANCH_WIDTH ** -0.5),
        'w_o': normal((n, d, d), d ** -0.5),
    }


def reference(x, c, ctx, c_ctx, w_mod, b_mod, norm_g, ffn1_up, ffn1_down, ffn2_up, ffn2_down,
              w_in, g_cq, w_uq, g_ckv, w_ukv, g_q_mla, g_k_mla, g_q_diff, g_k_diff, lam,
              g_subln, conv_w, w_br, w_o):
    b, s, d = x.shape
    ang_mla = axial_angles(s, MLA_ROPE)
    ang_diff = axial_angles(s, DIFF_DIM)
    cond = jax.nn.silu(c)
    cond_ctx = jax.nn.silu(c_ctx)
    xl, xc = x, ctx
    for l in range(DEPTH):
        last = l == DEPTH - 1
        mod_l = (cond @ w_mod[l] + b_mod[l]).reshape(b, N_MOD, 1, d)
        mod_c = (cond_ctx @ w_mod[l] + b_mod[l]).reshape(N_MOD, d)
        ml = [mod_l[:, i] for i in range(N_MOD)]
        mc = [mod_c[i] for i in range(N_MOD)]

        xl = xl + 0.5 * ml[2] * swiglu(modulate(xl, norm_g[l, 0], ml[0], ml[1]), ffn1_up[l], ffn1_down[l])
        xc = xc + 0.5 * mc[2] * swiglu(modulate(xc, norm_g[l, 0], mc[0], mc[1]), ffn1_up[l], ffn1_down[l])

        hl = modulate(xl, norm_g[l, 1], ml[3], ml[4])
        hc = modulate(xc, norm_g[l, 1], mc[3], mc[4])
        p = {'w_in': w_in[l], 'g_cq': g_cq[l], 'w_uq': w_uq[l], 'g_ckv': g_ckv[l], 'w_ukv': w_ukv[l],
             'g_q_mla': g_q_mla[l], 'g_k_mla': g_k_mla[l], 'g_q_diff': g_q_diff[l], 'g_k_diff': g_k_diff[l],
             'lam': lam[l], 'g_subln': g_subln[l], 'conv_w': conv_w[l], 'w_br': w_br[l], 'w_o': w_o[l]}
        lam_init = 0.8 - 0.6 * math.exp(-0.3 * l)
        oc, ol = mixer(hc, hl, ang_mla, ang_diff, p, lam_init, not last)
        xl = xl + ml[5] * ol

        xl = xl + 0.5 * ml[8] * swiglu(modulate(xl, norm_g[l, 2], ml[6], ml[7]), ffn2_up[l], ffn2_down[l])
        if not last:
            xc = xc + mc[5] * oc
            xc = xc + 0.5 * mc[8] * swiglu(modulate(xc, norm_g[l, 2], mc[6], mc[7]), ffn2_up[l], ffn2_down[l])
    return xl
```

```python
import math
import contextlib
import numpy as np
import concourse.bass as bass
import concourse.mybir as mybir
from concourse.bass_utils import run_bass_kernel_spmd

F32 = mybir.dt.float32
BF16 = mybir.dt.bfloat16
AF = mybir.ActivationFunctionType
ALU = mybir.AluOpType

DEPTH = 4
D = 1024
NCTX = 256
NLAT = 2048
NTOK = NCTX + NLAT
DFF = 2816
EPS = 1e-6
TT = [(0, 256), (256, 512), (768, 512), (1280, 512), (1792, 512)]
NSLOT = 21
SLOTW = 2304
NS = 128
NTAB = 384
SEM_CHUNK = 12000


class Res:
    __slots__ = ("name", "w", "r", "dsem", "dcnt", "excl")

    def __init__(self, name, excl=False):
        self.name = name
        self.w = None
        self.r = {}
        self.dsem = None
        self.dcnt = 0
        self.excl = excl


class V:
    __slots__ = ("ap", "res")

    def __init__(self, ap, res):
        self.ap = ap
        self.res = list(res) if isinstance(res, (list, tuple)) else [res]


class Eng:
    def __init__(self, name, h):
        self.name = name
        self.h = h
        self.sems = []
        self.nsig = 0
        self.waited = {}


class Sched:
    def __init__(self, nc, es):
        self.nc = nc
        self.es = es
        self.dry = False
        self.E = {"pe": Eng("pe", nc.tensor), "act": Eng("act", nc.scalar), "dve": Eng("dve", nc.vector),
                  "pool": Eng("pool", nc.gpsimd), "sp": Eng("sp", nc.sync)}
        self.ninst = 0
        self.nwait = 0

    def new_sem(self, name):
        return self.es.enter_context(self.nc.semaphore(name))

    def _sem_for(self, e, idx):
        ci = (idx - 1) // SEM_CHUNK
        while len(e.sems) <= ci:
            e.sems.append(self.new_sem(f"s_{e.name}_{len(e.sems)}"))
        return e.sems[ci], (idx - 1) % SEM_CHUNK + 1

    def _wait(self, e, deps, embed=False):
        need = []
        for ev in deps:
            if ev is None:
                continue
            if ev[0] == "E":
                _, name, idx = ev
                src = self.E[name]
                if name == e.name and name == "pe":
                    continue
                assert idx <= src.nsig, f"wait on pending signal {ev} by {e.name}"
                if e.waited.get(name, 0) >= idx:
                    continue
                e.waited[name] = idx
                need = [x for x in need if x[0] != name]
                need.append((name,) + self._sem_for(src, idx))
            else:
                _, owner, val = ev
                key = ("D", id(owner))
                if e.waited.get(key, 0) >= val:
                    continue
                e.waited[key] = val
                need = [x for x in need if x[0] != key]
                need.append((key, owner.dsem, val))
        last = None
        if embed and need:
            last = need.pop()
        for (_, sem, val) in need:
            e.h.wait_ge(sem, val)
            self.nwait += 1
        return last

    def op(self, eng, fn, reads, writes, sig=True):
        if self.dry:
            return
        rres = [r for v in reads if v is not None for r in v.res]
        wres = [r for v in writes if v is not None for r in v.res]
        e = self.E[eng]
        deps = []
        for r in rres:
            deps.append(r.w)
            if r.excl:
                deps.extend(v for k, v in r.r.items() if k != e.name)
        for w in wres:
            deps.append(w.w)
            deps.extend(w.r.values())
        emb = self._wait(e, deps, embed=True)
        inst = fn()
        if emb is not None:
            inst._wait_ge(emb[1], emb[2])
        self.ninst += 1
        if sig:
            e.nsig += 1
            sem, _ = self._sem_for(e, e.nsig)
            inst.then_inc(sem, 1)
            ev = ("E", e.name, e.nsig)
        else:
            ev = ("E", e.name, e.nsig + 1)
        for r in rres:
            r.r[e.name] = ev
        for w in wres:
            w.w = ev
            w.r = {}

    def dma(self, q, pairs, owner, reads=(), writes=()):
        if self.dry:
            return
        rres = [r for v in reads for r in v.res]
        wres = [r for v in writes for r in v.res]
        e = self.E[q]
        if owner.dsem is None:
            owner.dsem = self.new_sem(f"d_{owner.name}")
        deps = []
        for r in rres:
            deps.append(r.w)
        for w in wres:
            deps.append(w.w)
            deps.extend(w.r.values())
        emb = self._wait(e, deps, embed=True)
        for (o, i) in pairs:
            di = e.h.dma_start(out=o, in_=i)
            if emb is not None:
                di._wait_ge(emb[1], emb[2])
                emb = None
            di.then_inc(owner.dsem, 16)
            owner.dcnt += 16
            self.ninst += 1
        ev = ("D", owner, owner.dcnt)
        for r in rres:
            r.r[("D", id(owner))] = ev
        for w in wres:
            w.w = ev
            w.r = {}

    def wait_all(self, eng, views):
        if self.dry:
            return
        e = self.E[eng]
        deps = []
        for v in views:
            for r in v.res:
                deps.append(r.w)
                deps.extend(r.r.values())
        self._wait(e, deps)


class TilePool:
    def __init__(self, name, tiles):
        self.name = name
        self.tiles = tiles
        self.free = list(range(len(tiles)))
        self.live = set()

    def get(self):
        assert self.free, f"pool {self.name} exhausted"
        i = self.free.pop(0)
        self.live.add(i)
        t, r = self.tiles[i]
        return Tile(self, i, t, r)


class _Stop(Exception):
    pass


class Tile:
    __slots__ = ("pool", "i", "t", "r")

    def __init__(self, pool, i, t, r):
        self.pool, self.i, self.t, self.r = pool, i, t, r

    def v(self, ap=None):
        return V(self.t[:] if ap is None else ap, self.r)

    def done(self):
        self.pool.live.discard(self.i)
        self.pool.free.append(self.i)


def build_program(n_layers=DEPTH, debug_ctx=False, stop_after=None):
    nc = bass.Bass("TRN2", target_bir_lowering=False)
    din = {}

    def inp(name, shape):
        din[name] = nc.dram_tensor(name, list(shape), F32, kind="ExternalInput").ap()
        return din[name]

    xT_d = inp("xT", [D, NLAT])
    ctxT_d = inp("ctxT", [D, NCTX])
    cc_d = inp("cc", [128, 16])
    smalls_d = inp("smalls", [128, DEPTH * NS])
    tabs_d = inp("tabs", [128, NTAB])
    w_mod_d = inp("w_mod", [DEPTH, D, 9 * D])
    f1u_d = inp("ffn1_up", [DEPTH, D, 2 * DFF])
    f1d_d = inp("ffn1_down", [DEPTH, DFF, D])
    f2u_d = inp("ffn2_up", [DEPTH, D, 2 * DFF])
    f2d_d = inp("ffn2_down", [DEPTH, DFF, D])
    w_in_d = inp("w_in", [DEPTH, D, 6816])
    w_krP_d = inp("w_krP", [DEPTH, D, 96])
    w_dqP_d = inp("w_dqP", [DEPTH, D, 512])
    w_dkP_d = inp("w_dkP", [DEPTH, D, 512])
    w_uq_d = inp("w_uq", [DEPTH, 384, 768])
    w_uqP_d = inp("w_uqP", [DEPTH, 384, 768])
    w_ukv_d = inp("w_ukv", [DEPTH, 256, 1024])
    w_br_d = inp("w_br", [DEPTH, 3, 512, D])
    w_o_d = inp("w_o", [DEPTH, D, D])
    outT_d = nc.dram_tensor("outT", [D, NLAT], F32, kind="ExternalOutput").ap()
    outC_d = nc.dram_tensor("outC", [D, NCTX], F32, kind="ExternalOutput").ap() if debug_ctx else None

    es = contextlib.ExitStack()
    with es:
        S = Sched(nc, es)

        def sbuf(name, shape, dt):
            return es.enter_context(nc.sbuf_tensor(name, list(shape), dt))

        XT = sbuf("XT", [128, 8, NTOK], F32)
        Xres = [[Res(f"X{c}_{t}") for t in range(5)] for c in range(8)]
        AR = sbuf("AR", [128, NSLOT, SLOTW], BF16)
        ARres = [[Res(f"A{s}_{t}") for t in range(5)] for s in range(NSLOT)]
        SM = sbuf("SM", [128, DEPTH * NS], F32)
        SMr = Res("SM")
        TAB = sbuf("TAB", [128, NTAB], F32)
        TABr = Res("TAB")
        CCf = sbuf("CCf", [128, 16], F32)
        CCfr = Res("CCf")
        COND = sbuf("COND", [128, 16], BF16)
        CONDr = Res("COND")
        ONES = sbuf("ONES", [128, 128], BF16)
        ONESr = Res("ONES")
        ONESF = sbuf("ONESF", [128, 128], F32)
        ONESFr = Res("ONESF")
        EPSC = sbuf("EPSC", [128, 4], F32)
        EPSCr = Res("EPSC")
        MODT = [sbuf(f"MODT{i}", [128, 144], F32) for i in range(2)]
        MODTr = [Res(f"MODT{i}") for i in range(2)]
        DER = sbuf("DER", [128, 96], F32)
        DERr = Res("DER")
        MISC = sbuf("MISC", [128, 160], F32)
        MISCr = [Res(f"MISC{i}") for i in range(8)]

        BLK2 = sbuf("BLK2", [128, 2], BF16)
        BLKONES = sbuf("BLKONES", [128, 128], BF16)
        MASKR = sbuf("MASKR", [128, 1], BF16)
        CONSTr = Res("CONST")
        QM = [(sbuf(f"QM{i}", [128, 512], BF16), Res(f"QM{i}")) for i in range(2)]
        QD = [((sbuf(f"QA{i}", [128, 512], BF16), Res(f"QA{i}")), (sbuf(f"QB{i}", [128, 512], BF16), Res(f"QB{i}"))) for i in range(2)]
        ps_tiles = []
        for i in range(8):
            t = es.enter_context(nc.psum_tensor(f"ps{i}", [128, 512], F32))
            ps_tiles.append((t, Res(f"ps{i}", excl=True)))
        PS = TilePool("psum", ps_tiles)
        FP = TilePool("F", [(sbuf(f"F{i}", [128, 512], F32), Res(f"F{i}")) for i in range(6)])
        BP = TilePool("B", [(sbuf(f"B{i}", [128, 512], BF16), Res(f"B{i}")) for i in range(7)])

        arena_free = list(range(NSLOT))

        def aalloc(n=1):
            assert len(arena_free) >= n, f"arena exhausted (need {n}, have {len(arena_free)})"
            out = [arena_free.pop(0) for _ in range(n)]
            return out

        def afree(slots):
            for s in slots:
                arena_free.append(s)

        def slot_v(s, t, p0=0, p1=128):
            a, n = TT[t]
            return V(AR[p0:p1, s, a:a + n], ARres[s][t])

        def slot_cols(s, c0, n, p0=0, p1=128):
            res = [ARres[s][t] for t, (a, m) in enumerate(TT) if a < c0 + n and c0 < a + m]
            return V(AR[p0:p1, s, c0:c0 + n], res)

        def slot_all(s):
            return V(None, ARres[s])

        def xv(c, t):
            a, n = TT[t]
            return V(XT[:, c, a:a + n], Xres[c][t])

        class WS:
            descs = []
            nxt = 0
            outstanding = 0
            cur_cap = 8
            req = 0

        class Piece:
            __slots__ = ("slot", "idx")

            def view(self, k, n, off=0):
                return AR[:, self.slot, off:off + k * n].rearrange("p (k n) -> p k n", n=n)

            def v(self, ap):
                return V(ap, ARres[self.slot])

        def w_issue():
            while WS.nxt < len(WS.descs):
                pairs, cap = WS.descs[WS.nxt][0], WS.descs[WS.nxt][1]
                if WS.outstanding + 1 > min(cap, WS.cur_cap) or not arena_free:
                    break
                s = arena_free.pop(0)
                dp = []
                for (off, k, n, src) in pairs:
                    dp.append((AR[:, s, off:off + k * n].rearrange("p (k n) -> p k n", n=n), src))
                S.dma("pool", dp, ARres[s][0], writes=[slot_all(s)])
                WS.descs[WS.nxt] = (pairs, cap, s)
                WS.nxt += 1
                WS.outstanding += 1

        def w_get(pairs, cap=6):
            p = Piece()
            p.idx = WS.req
            WS.req += 1
            if S.dry:
                WS.descs.append((pairs, cap))
                p.slot = 0
                return p
            w_issue()
            assert p.idx < WS.nxt, f"weight piece {p.idx} could not be issued (outstanding={WS.outstanding}, free={len(arena_free)})"
            p.slot = WS.descs[p.idx][2]
            return p

        def w_free(p):
            if S.dry:
                return
            arena_free.append(p.slot)
            WS.outstanding -= 1
            w_issue()

        def mm(out, lhsT, rhs, start, stop, sig=None):
            S.op("pe", lambda: nc.tensor.matmul(out.ap, lhsT=lhsT.ap, rhs=rhs.ap, start=start, stop=stop),
                 [lhsT, rhs], [out], sig=(stop if sig is None else sig))

        def _a(x):
            return x.ap if isinstance(x, V) else x

        def _v(x):
            return x if isinstance(x, V) else None

        def act(out, in_, func, bias=None, scale=1.0):
            kw = {}
            if bias is not None:
                kw["bias"] = _a(bias)
            S.op("act", lambda: nc.scalar.activation(out=out.ap, in_=in_.ap, func=func, scale=_a(scale), **kw),
                 [in_, _v(bias), _v(scale)], [out])

        def tt_(out, in0, in1, op, eng="dve"):
            h = nc.vector if eng == "dve" else nc.gpsimd
            S.op(eng, lambda: h.tensor_tensor(out=out.ap, in0=in0.ap, in1=in1.ap, op=op), [in0, in1], [out])

        def stt(out, in0, scalar, in1, op0, op1, eng="dve"):
            h = nc.vector if eng == "dve" else nc.gpsimd
            S.op(eng, lambda: h.scalar_tensor_tensor(out=out.ap, in0=in0.ap, scalar=_a(scalar), in1=in1.ap, op0=op0, op1=op1),
                 [in0, _v(scalar), in1], [out])

        def ts(out, in0, s1, s2, op0, op1, eng="dve"):
            h = nc.vector if eng == "dve" else nc.gpsimd
            S.op(eng, lambda: h.tensor_scalar(out=out.ap, in0=in0.ap, scalar1=_a(s1), scalar2=_a(s2), op0=op0, op1=op1),
                 [in0, _v(s1), _v(s2)], [out])

        def recip(out, in_):
            S.op("dve", lambda: nc.vector.reciprocal(out=out.ap, in_=in_.ap), [in_], [out])

        def vcopy(out, in_):
            S.op("dve", lambda: nc.vector.tensor_copy(out=out.ap, in_=in_.ap), [in_], [out])

        def memset(t_ap, res, val):
            S.op("pool", lambda: nc.gpsimd.memset(t_ap, val), [], [V(None, res)])

        def smcol(l, c, p0=0, p1=128, n=1):
            return V(SM[p0:p1, l * NS + c:l * NS + c + n], SMr)

        def misc(i, c, n=1, p0=0, p1=128):
            return V(MISC[p0:p1, c:c + n], MISCr[i])

        def rstd_from(pss, n, P, scale, epscol, p0=0):
            rs = FP.get()
            rv = V(rs.t[p0:P, 0:n], rs.r)
            act(rv, pss, AF.Ln, bias=V(EPSC[p0:P, epscol:epscol + 1], EPSCr), scale=scale)
            act(rv, rv, AF.Exp, scale=-0.5)
            return rs

        def arecip(out, in_):
            act(out, in_, AF.Ln)
            act(out, out, AF.Exp, scale=-1.0)


        def program():
            WS.req = 0
            S.dma("sp", [(SM[:], smalls_d[:, :])], SMr, writes=[V(None, SMr)])
            S.dma("sp", [(TAB[:], tabs_d[:, :])], TABr, writes=[V(None, TABr)])
            S.dma("sp", [(CCf[:], cc_d[:, :])], CCfr, writes=[V(None, CCfr)])
            memset(ONES[:], ONESr, 1.0)
            memset(ONESF[:], ONESFr, 1.0)
            memset(EPSC[:, 0:1], EPSCr, EPS)
            memset(EPSC[:, 1:2], EPSCr, 96 * EPS)
            memset(EPSC[:, 2:3], EPSCr, 64 * EPS)
            memset(EPSC[:, 3:4], EPSCr, 0.0)
            memset(BLK2[:], CONSTr, 0.0)
            memset(BLK2[0:64, 0:1], CONSTr, 1.0)
            memset(BLK2[64:128, 1:2], CONSTr, 1.0)
            memset(BLKONES[:], CONSTr, 0.0)
            memset(BLKONES[0:64, 0:64], CONSTr, 1.0)
            memset(BLKONES[64:128, 64:128], CONSTr, 1.0)
            memset(MASKR[:], CONSTr, 0.0)
            memset(MASKR[64:96, :], CONSTr, 1.0)
            for (qt__, qr__) in QM + [x for pr_ in QD for x in pr_]:
                memset(qt__[:], qr__, 0.0)
            for c in range(8):
                S.dma("sp", [(XT[:, c, NCTX:NTOK], xT_d[c * 128:(c + 1) * 128, :])], Xres[c][1],
                      writes=[V(None, Xres[c][1:5])])
                S.dma("sp", [(XT[:, c, 0:NCTX], ctxT_d[c * 128:(c + 1) * 128, :])], Xres[c][0],
                      writes=[V(None, Xres[c][0])])
            act(V(COND[:], CONDr), V(CCf[:], CCfr), AF.Silu)

            Hs = aalloc(8)

            def hv(c, t):
                return slot_v(Hs[c], t)

            mod_state = {}

            def mod_begin(l):
                pm = PS.get()
                mod_state[l] = (pm, 0)

            def mod_pieces(l, n):
                pm, done = mod_state[l]
                pmv = pm.t[:, 0:144].rearrange("p (c w) -> p c w", w=2)
                wv = w_mod_d[l].rearrange("(kc p) n -> p kc n", p=128)
                for pi in range(done, min(36, done + n)):
                    pc = w_get([(0, 8, 256, wv[:, :, pi * 256:(pi + 1) * 256])], cap=10)
                    pvw = pc.view(8, 256)
                    for j in range(2):
                        col = 2 * pi + j
                        for kc in range(8):
                            mm(V(pmv[:, col, :], pm.r), pc.v(pvw[:, kc, j * 128:(j + 1) * 128]),
                               V(COND[:, 2 * kc:2 * kc + 2], CONDr), kc == 0, kc == 7)
                    w_free(pc)
                mod_state[l] = (pm, min(36, done + n))

            def mod_finish(l):
                pm, done = mod_state[l]
                assert done == 36
                mt, mr = MODT[l % 2], MODTr[l % 2]
                tt_(V(mt[:, 0:144].rearrange("p (c w) -> p c w", w=2), mr),
                    V(pm.t[:, 0:144].rearrange("p (c w) -> p c w", w=2), pm.r),
                    V(SM[:, l * NS:l * NS + 72].unsqueeze(2).broadcast_to([128, 72, 2]), SMr), ALU.add)
                pm.done()

            def modcol(l, i, c, w):
                k = ((i * 8 + c) * 2) + w
                return V(MODT[l % 2][:, k:k + 1], MODTr[l % 2])

            def derive(l):
                mt, mr = MODT[l % 2], MODTr[l % 2]
                for j in range(3):
                    sc = V(mt[:, (3 * j + 1) * 16:(3 * j + 1) * 16 + 16].rearrange("p (c w) -> p c w", w=2), mr)
                    gn = V(SM[:, l * NS + 72 + j * 8:l * NS + 72 + j * 8 + 8].unsqueeze(2).broadcast_to([128, 8, 2]), SMr)
                    stt(V(DER[:, j * 16:(j + 1) * 16].rearrange("p (c w) -> p c w", w=2), DERr), sc, 1.0, gn, ALU.add, ALU.mult)
                for k, j in ((3, 0), (4, 2)):
                    g = V(mt[:, (3 * j + 2) * 16:(3 * j + 2) * 16 + 16], mr)
                    ts(V(DER[:, k * 16:(k + 1) * 16], DERr), g, 0.5, 0.0, ALU.mult, ALU.add)

            def gs_col(j, c, w):
                k = j * 16 + c * 2 + w
                return V(DER[:, k:k + 1], DERr)

            def hg_col(j, c, w):
                k = (3 if j == 0 else 4) * 16 + c * 2 + w
                return V(DER[:, k:k + 1], DERr)

            def compute_H(l, j, tiles):
                for t in tiles:
                    a, n = TT[t]
                    w = 1 if t == 0 else 0
                    pss = PS.get()
                    pv = V(pss.t[:, 0:n], pss.r)
                    for c in range(8):
                        sq = BP.get()
                        sv = V(sq.t[:, 0:n], sq.r)
                        act(sv, xv(c, t), AF.Square)
                        mm(pv, V(ONES[:, 0:128], ONESr), sv, c == 0, c == 7, sig=True)
                        sq.done()
                    rs = rstd_from(pv, n, 128, 1.0 / D, 0)
                    pss.done()
                    rv = V(rs.t[:, 0:n], rs.r)
                    for c in range(8):
                        tmp = FP.get()
                        tv = V(tmp.t[:, 0:n], tmp.r)
                        stt(tv, xv(c, t), gs_col(j, c, w), rv, ALU.mult, ALU.mult)
                        act(hv(c, t), tv, AF.Identity, bias=modcol(l, 3 * j, c, w))
                        tmp.done()
                    rs.done()

            def ffn(l, j, up_d, down_d, tiles, interleave=None, h_ready=False, tail=None):
                if not h_ready:
                    compute_H(l, j, tiles)
                upv = up_d[l].rearrange("(kc p) n -> p kc n", p=128)
                for g in range(11):
                    pa = w_get([(0, 8, 256, upv[:, :, g * 256:(g + 1) * 256])], cap=12)
                    pb = w_get([(0, 8, 256, upv[:, :, DFF + g * 256:DFF + (g + 1) * 256])], cap=12)
                    pd = w_get([(0, 2, 1024, down_d[l][g * 256:(g + 1) * 256, :].rearrange("(fc p) n -> p fc n", p=128))], cap=12)
                    va, vb, vd = pa.view(8, 256), pb.view(8, 256), pd.view(2, 1024)
                    def up_part(t):
                        a, n = TT[t]
                        gts = []
                        for i in range(2):
                            qa, qb = PS.get(), PS.get()
                            qav, qbv = V(qa.t[:, 0:n], qa.r), V(qb.t[:, 0:n], qb.r)
                            for kc in range(8):
                                mm(qav, pa.v(va[:, kc, i * 128:(i + 1) * 128]), hv(kc, t), kc == 0, kc == 7)
                            for kc in range(8):
                                mm(qbv, pb.v(vb[:, kc, i * 128:(i + 1) * 128]), hv(kc, t), kc == 0, kc == 7)
                            sa = FP.get()
                            sav = V(sa.t[:, 0:n], sa.r)
                            act(sav, qav, AF.Silu)
                            qa.done()
                            gt = BP.get()
                            tt_(V(gt.t[:, 0:n], gt.r), qbv, sav, ALU.mult)
                            qb.done()
                            sa.done()
                            gts.append(gt)
                        return gts

                    def down_part(t, gts):
                        a, n = TT[t]
                        w = 1 if t == 0 else 0
                        for dd in range(8):
                            qd = PS.get()
                            qdv = V(qd.t[:, 0:n], qd.r)
                            for i in range(2):
                                mm(qdv, pd.v(vd[:, i, dd * 128:(dd + 1) * 128]), V(gts[i].t[:, 0:n], gts[i].r), i == 0, i == 1)
                            stt(xv(dd, t), qdv, hg_col(j, dd, w), xv(dd, t), ALU.mult, ALU.add)
                            qd.done()
                        for gt in gts:
                            gt.done()

                    prev = None
                    done_tiles = []
                    for t in tiles:
                        gts = up_part(t)
                        if prev is not None:
                            down_part(*prev)
                            done_tiles.append(prev[0])
                            if tail is not None and g == 10 and len(done_tiles) >= 2:
                                tail(done_tiles[-2])
                        prev = (t, gts)
                    down_part(*prev)
                    done_tiles.append(prev[0])
                    if tail is not None and g == 10:
                        if len(done_tiles) >= 2:
                            tail(done_tiles[-2])
                        tail(done_tiles[-1])
                    w_free(pa)
                    w_free(pb)
                    w_free(pd)
                    if interleave is not None:
                        interleave(g)

            def rope(out, x, xP, g, gP, tab0, p0, p1, t, final=None):
                r0 = 8 * (t - 1)
                P = p1 - p0
                CR = V(TAB[p0:p1, tab0 + r0:tab0 + r0 + 8].unsqueeze(2).broadcast_to([P, 8, 64]), TABr)
                CC = V(TAB[p0:p1, tab0 + 32:tab0 + 96].unsqueeze(1).broadcast_to([P, 8, 64]), TABr)
                SR = V(TAB[p0:p1, tab0 + 96 + r0:tab0 + 96 + r0 + 8].unsqueeze(2).broadcast_to([P, 8, 64]), TABr)
                SC = V(TAB[p0:p1, tab0 + 128:tab0 + 192].unsqueeze(1).broadcast_to([P, 8, 64]), TABr)

                def v3(vv):
                    return V(vv.ap.rearrange("p (r c) -> p r c", c=64), vv.res)
                fa, fb = FP.get(), FP.get()
                a = V(fa.t[p0:p1, :], fa.r)
                b = V(fb.t[p0:p1, :], fb.r)
                stt(v3(a), v3(x), g, CR, ALU.mult, ALU.mult)
                tt_(v3(a), v3(a), CC, ALU.mult)
                stt(v3(b), v3(xP), gP, SR, ALU.mult, ALU.mult)
                tt_(v3(b), v3(b), SC, ALU.mult)
                if final is None:
                    for (o_, q0, q1) in out:
                        tt_(o_, V(fa.t[q0:q1, :], fa.r), V(fb.t[q0:q1, :], fb.r), ALU.add)
                else:
                    tt_(a, a, b, ALU.add)
                    for (o_, q0, q1) in out:
                        tt_(o_, V(fa.t[q0:q1, :], fa.r), V(final[0][q0:q1, 0:512], final[1]), ALU.mult)
                fa.done()
                fb.done()

            def mixer(l, last, ffn2_follows=True):
                lam_init = 0.8 - 0.6 * math.exp(-0.3 * l)

                def chk(k):
                    if stop_after == (l, f'dbg{k}'):
                        raise _Stop()
                qtiles = [1, 2, 3, 4] if last else [0, 1, 2, 3, 4]
                alltiles = [0, 1, 2, 3, 4]
                win = w_in_d[l].rearrange("(kc p) n -> p kc n", p=128)
                WS.cur_cap = 5

                prod = misc(0, 0, 2, 0, 64)
                tt_(V(MISC[0:64, 0:1], MISCr[0]), smcol(l, 124, 0, 64), smcol(l, 125, 0, 64), ALU.mult)
                tt_(V(MISC[0:64, 1:2], MISCr[0]), smcol(l, 126, 0, 64), smcol(l, 127, 0, 64), ALU.mult)
                pl = PS.get()
                mm(V(pl.t[:, 0:2], pl.r), V(ONESF[0:64, 0:128], ONESFr), prod, True, True)
                act(misc(0, 2, 2), V(pl.t[:, 0:2], pl.r), AF.Exp)
                pl.done()
                stt(misc(1, 4), misc(0, 3), -lam_init, misc(0, 2), ALU.add, ALU.subtract)
                ts(misc(1, 5), smcol(l, 111), 1.0 - lam_init, 0.0, ALU.mult, ALU.add)
                neglam = misc(1, 4)
                gsub = misc(1, 5)

                cq_s, ckv_s = aalloc(3), aalloc(2)
                kt_ss = aalloc(2)
                pq0 = w_get([(0, 8, 256, win[:, :, 0:256])])
                pq1 = w_get([(0, 8, 128, win[:, :, 256:384])])
                pkv = w_get([(0, 8, 256, win[:, :, 384:640])])
                pkr = w_get([(0, 8, 96, win[:, :, 576:672]),
                             (768, 8, 96, w_krP_d[l].rearrange("(kc p) n -> p kc n", p=128))])
                vq0, vq1, vkv = pq0.view(8, 256), pq1.view(8, 128), pkv.view(8, 256)
                vkr, vkrP = pkr.view(8, 96), pkr.view(8, 96, 768)

                def lat_norm(lhs_of, nchunks, dst_slots, gcol0, Dn, t):
                    a, n = TT[t]
                    banks = []
                    for cc_ in range(nchunks):
                        pb_ = PS.get()
                        bv = V(pb_.t[:, 0:n], pb_.r)
                        for kc in range(8):
                            mm(bv, lhs_of(cc_, kc), hv(kc, t), kc == 0, kc == 7)
                        banks.append(pb_)
                    pss = PS.get()
                    pv = V(pss.t[:, 0:n], pss.r)
                    for cc_ in range(nchunks):
                        sq = BP.get()
                        sv = V(sq.t[:, 0:n], sq.r)
                        act(sv, V(banks[cc_].t[:, 0:n], banks[cc_].r), AF.Square)
                        mm(pv, V(ONES[:, 0:128], ONESr), sv, cc_ == 0, cc_ == nchunks - 1)
                        sq.done()
                    rs = rstd_from(pv, n, 128, 1.0 / Dn, 0)
                    pss.done()
                    for cc_ in range(nchunks):
                        stt(slot_v(dst_slots[cc_], t), V(banks[cc_].t[:, 0:n], banks[cc_].r), smcol(l, gcol0 + cc_),
                            V(rs.t[:, 0:n], rs.r), ALU.mult, ALU.mult)
                        banks[cc_].done()
                    rs.done()

                for t in alltiles:
                    lat_norm(lambda cc_, kc: (pq0.v(vq0[:, kc, cc_ * 128:(cc_ + 1) * 128]) if cc_ < 2 else pq1.v(vq1[:, kc, 0:128])),
                             3, cq_s, 96, 384, t)
                for t in alltiles:
                    lat_norm(lambda cc_, kc: pkv.v(vkv[:, kc, cc_ * 128:(cc_ + 1) * 128]), 2, ckv_s, 99, 256, t)
                for kt_s in kt_ss:
                    S.op("pool", lambda kt_s=kt_s: nc.gpsimd.memset(AR[96:128, kt_s, :], 0.0), [], [slot_all(kt_s)])
                pssr = PS.get()
                for t in alltiles:
                    a, n = TT[t]
                    pr, pp = PS.get(), PS.get()
                    prv, ppv = V(pr.t[0:96, 0:n], pr.r), V(pp.t[0:96, 0:n], pp.r)
                    for kc in range(8):
                        mm(prv, pkr.v(vkr[:, kc, :]), hv(kc, t), kc == 0, kc == 7)
                    if t > 0:
                        for kc in range(8):
                            mm(ppv, pkr.v(vkrP[:, kc, :]), hv(kc, t), kc == 0, kc == 7)
                    sq = BP.get()
                    act(V(sq.t[0:96, 0:n], sq.r), prv, AF.Square)
                    for sub in range(n // 128):
                        kt = a // 128 + sub
                        mm(V(pssr.t[:, kt:kt + 1], pssr.r), V(sq.t[0:96, sub * 128:(sub + 1) * 128], sq.r),
                           V(MASKR[0:96, 0:1], CONSTr), True, True)
                    sq.done()
                    r64 = V(pr.t[64:96, 0:n], pr.r)
                    if t == 0:
                        for kt_s in kt_ss:
                            act(slot_v(kt_s, t, 64, 96), r64, AF.Identity, scale=smcol(l, 104, 64, 96))
                    else:
                        rope([(slot_v(kt_s, t, 64, 96), 64, 96) for kt_s in kt_ss], r64, V(pp.t[64:96, 0:n], pp.r),
                             smcol(l, 104, 64, 96), smcol(l, 106, 64, 96), 0, 64, 96, t)
                    pr.done()
                    pp.done()
                ssr = misc(2, 8, 18)
                vcopy(ssr, V(pssr.t[:, 0:18], pssr.r))
                pssr.done()
                for p_ in (pq0, pq1, pkv, pkr):
                    w_free(p_)
                if stop_after == (l, 'm1'):
                    raise _Stop()
                afree(Hs)

                ymla = aalloc(4)
                va_s = aalloc(2)
                vaug = [AR[:, va_s[i], :].rearrange("p (k e) -> p k e", e=128) for i in range(2)]
                S.op("pool", lambda: nc.gpsimd.memset(vaug[0][:, :, 64:128], 1.0), [], [slot_all(va_s[0])])
                S.op("pool", lambda: nc.gpsimd.memset(vaug[1][:, :, 0:64], 1.0), [], [slot_all(va_s[1])])
                WS.cur_cap = 8
                uq = w_uq_d[l].rearrange("(kc p) n -> p kc n", p=128)
                uqP = w_uqP_d[l].rearrange("(kc p) n -> p kc n", p=128)
                ukv = w_ukv_d[l].rearrange("(kc p) n -> p kc n", p=128)
                qcount = 0
                LAG = 3
                mla_pieces = {}

                def mla_setup(h):
                    par = h % 2
                    kt_s = kt_ss[par]
                    wq = w_get([(0, 3, 96, uq[:, :, h * 96:(h + 1) * 96]), (288, 3, 96, uqP[:, :, h * 96:(h + 1) * 96])])
                    wkv = w_get([(0, 2, 128, ukv[:, :, h * 128:(h + 1) * 128])])
                    mla_pieces[h] = (wq, wkv)
                    vwkv = wkv.view(2, 128)
                    pssk = PS.get()
                    for t in alltiles:
                        a, n = TT[t]
                        pk = PS.get()
                        pkv_ = V(pk.t[0:64, 0:n], pk.r)
                        for kc in range(2):
                            mm(pkv_, wkv.v(vwkv[:, kc, 0:64]), slot_v(ckv_s[kc], t), kc == 0, kc == 1)
                        sq = BP.get()
                        act(V(sq.t[0:64, 0:n], sq.r), pkv_, AF.Square)
                        for sub in range(n // 128):
                            kt = a // 128 + sub
                            mm(V(pssk.t[:, kt:kt + 1], pssk.r), V(sq.t[0:64, sub * 128:(sub + 1) * 128], sq.r),
                               V(ONES[0:64, 0:1], ONESr), True, True)
                        sq.done()
                        act(slot_v(kt_s, t, 0, 64), pkv_, AF.Identity, scale=smcol(l, 104, 0, 64))
                        pk.done()
                    rk = misc(3 + par, 32 + 18 * par, 18)
                    tt_(rk, V(pssk.t[:, 0:18], pssk.r), ssr, ALU.add)
                    pssk.done()
                    act(rk, rk, AF.Ln, bias=V(EPSC[:, 1:2], EPSCr))
                    act(rk, rk, AF.Exp, scale=-0.5)
                    off = 0 if par == 0 else 64
                    for k0 in range(0, 18, 8):
                        nk = min(8, 18 - k0)
                        pvb = PS.get()
                        pv3 = pvb.t[:, 0:nk * 64].rearrange("p (k e) -> p k e", e=64)
                        for i in range(nk):
                            kt = k0 + i
                            for kc in range(2):
                                mm(V(pv3[:, i, :], pvb.r), slot_cols(ckv_s[kc], kt * 128, 128), wkv.v(vwkv[:, kc, 64:128]),
                                   kc == 0, kc == 1)
                        res = [ARres[va_s[par]][t] for t, (a, m) in enumerate(TT) if a < (k0 + nk) * 128 and k0 * 128 < a + m]
                        vcopy(V(vaug[par][:, k0:k0 + nk, off:off + 64], res), V(pv3, pvb.r))
                        pvb.done()

                def mla_prep(h, t, qi):
                    wq = mla_pieces[h][0]
                    vwq, vwqP = wq.view(3, 96), wq.view(3, 96, 288)
                    a, n = TT[t]
                    pn = PS.get()
                    pnv = V(pn.t[0:96, 0:n], pn.r)
                    for kc in range(3):
                        mm(pnv, wq.v(vwq[:, kc, :]), slot_v(cq_s[kc], t), kc == 0, kc == 2)
                    if t > 0:
                        pp = PS.get()
                        ppv = V(pp.t[0:96, 0:n], pp.r)
                        for kc in range(3):
                            mm(ppv, wq.v(vwqP[:, kc, :]), slot_v(cq_s[kc], t), kc == 0, kc == 2)
                    sqn = BP.get()
                    act(V(sqn.t[0:96, 0:n], sqn.r), pnv, AF.Square)
                    pss = PS.get()
                    pssv = V(pss.t[0:96, 0:n], pss.r)
                    mm(pssv, V(ONES[0:96, 0:96], ONESr), V(sqn.t[0:96, 0:n], sqn.r), True, True)
                    sqn.done()
                    rs = rstd_from(pssv, n, 96, 1.0 / 96, 0)
                    pss.done()
                    qt_, qr_ = QM[qi % 2]
                    if t == 0:
                        stt(V(qt_[0:96, 0:n], qr_), pnv, smcol(l, 101, 0, 96), V(rs.t[0:96, 0:n], rs.r), ALU.mult, ALU.mult)
                    else:
                        stt(V(qt_[0:64, 0:n], qr_), V(pn.t[0:64, 0:n], pn.r), smcol(l, 101, 0, 64), V(rs.t[0:64, 0:n], rs.r), ALU.mult, ALU.mult)
                        rope([(V(qt_[64:96, 0:n], qr_), 64, 96)], V(pn.t[64:96, 0:n], pn.r), V(pp.t[64:96, 0:n], pp.r),
                             smcol(l, 101, 64, 96), smcol(l, 103, 64, 96), 0, 64, 96, t, final=(rs.t, rs.r))
                        pp.done()
                    pn.done()
                    rs.done()
                    return V(qt_[:, 0:n], qr_)

                def mla_attend(h, t, qv):
                    par = h % 2
                    kt_s = kt_ss[par]
                    a, n = TT[t]
                    acc = PS.get()
                    accv = V(acc.t[:, 0:n], acc.r)
                    kts = list(range(18)) if t > 0 else [0, 1]
                    pend = []
                    for i in range(len(kts) + LAG):
                        if i < len(kts):
                            kt = kts[i]
                            sps = PS.get()
                            sv = V(sps.t[:, 0:n], sps.r)
                            mm(sv, slot_cols(kt_s, kt * 128, 128), qv, True, True)
                            pt = BP.get()
                            act(V(pt.t[:, 0:n], pt.r), sv, AF.Exp, scale=V(MISC[:, 32 + 18 * par + kt:32 + 18 * par + kt + 1], MISCr[3 + par]))
                            sps.done()
                            pend.append((kt, pt))
                        if i >= LAG:
                            kt, pt = pend.pop(0)
                            res = [ARres[va_s[par]][tt2] for tt2, (a2, m2) in enumerate(TT) if a2 <= kt * 128 < a2 + m2]
                            mm(accv, V(vaug[par][:, kt, :], res), V(pt.t[:, 0:n], pt.r), kt == kts[0], kt == kts[-1])
                            pt.done()
                    rc = FP.get()
                    if par == 0:
                        recip(V(rc.t[64:128, 0:n], rc.r), V(acc.t[64:128, 0:n], acc.r))
                        tt_(slot_v(ymla[h // 2], t, 0, 64), V(acc.t[0:64, 0:n], acc.r), V(rc.t[64:128, 0:n], rc.r), ALU.mult)
                    else:
                        recip(V(rc.t[0:64, 0:n], rc.r), V(acc.t[0:64, 0:n], acc.r))
                        tt_(slot_v(ymla[h // 2], t, 64, 128), V(acc.t[64:128, 0:n], acc.r), V(rc.t[0:64, 0:n], rc.r), ALU.mult)
                    rc.done()
                    acc.done()

                units = [(h, t) for h in range(8) for t in qtiles]
                mla_setup(0)
                qcur = mla_prep(units[0][0], units[0][1], 0)
                for ui, (h, t) in enumerate(units):
                    qnext = None
                    if ui + 1 < len(units):
                        h2, t2 = units[ui + 1]
                        if h2 != h:
                            mla_setup(h2)
                        qnext = mla_prep(h2, t2, ui + 1)
                    mla_attend(h, t, qcur)
                    if ui + 1 == len(units) or units[ui + 1][0] != h:
                        w_free(mla_pieces[h][0])
                        w_free(mla_pieces[h][1])
                    qcur = qnext
                if stop_after == (l, 'mla'):
                    raise _Stop()
                afree(kt_ss + va_s + cq_s + ckv_s)
                WS.cur_cap = 3

                Hs[:] = aalloc(8)
                compute_H(l, 1, alltiles)
                ydiff = aalloc(4)
                dk_s = aalloc(1)[0]
                dv_s = aalloc(1)[0]
                dvv = AR[:, dv_s, :].rearrange("p (k e) -> p k e", e=128)
                dqP = w_dqP_d[l].rearrange("(kc p) n -> p kc n", p=128)
                dkP = w_dkP_d[l].rearrange("(kc p) n -> p kc n", p=128)
                for h in range(4):
                    wdq = w_get([(0, 8, 128, win[:, :, 672 + 128 * h:672 + 128 * (h + 1)]), (1024, 8, 128, dqP[:, :, 128 * h:128 * (h + 1)])], cap=3)
                    wdk = w_get([(0, 8, 128, win[:, :, 1184 + 128 * h:1184 + 128 * (h + 1)]), (1024, 8, 128, dkP[:, :, 128 * h:128 * (h + 1)])], cap=3)
                    wdv = w_get([(0, 8, 128, win[:, :, 1696 + 128 * h:1696 + 128 * (h + 1)])], cap=3)
                    vdq, vdqP = wdq.view(8, 128), wdq.view(8, 128, 1024)
                    vdk, vdkP = wdk.view(8, 128), wdk.view(8, 128, 1024)
                    vdv = wdv.view(8, 128)
                    pssd = PS.get()
                    for t in alltiles:
                        a, n = TT[t]
                        pk = PS.get()
                        pkv_ = V(pk.t[:, 0:n], pk.r)
                        for kc in range(8):
                            mm(pkv_, wdk.v(vdk[:, kc, :]), hv(kc, t), kc == 0, kc == 7)
                        if t > 0:
                            pp = PS.get()
                            ppv = V(pp.t[:, 0:n], pp.r)
                            for kc in range(8):
                                mm(ppv, wdk.v(vdkP[:, kc, :]), hv(kc, t), kc == 0, kc == 7)
                        sq = BP.get()
                        act(V(sq.t[:, 0:n], sq.r), pkv_, AF.Square)
                        for sub in range(n // 128):
                            kt = a // 128 + sub
                            mm(V(pssd.t[:, 2 * kt:2 * kt + 2], pssd.r), V(sq.t[:, sub * 128:(sub + 1) * 128], sq.r),
                               V(BLK2[:, 0:2], CONSTr), True, True)
                        sq.done()
                        if t == 0:
                            act(slot_v(dk_s, t), pkv_, AF.Identity, scale=smcol(l, 109))
                        else:
                            rope([(slot_v(dk_s, t), 0, 128)], pkv_, ppv, smcol(l, 109), smcol(l, 110), 192, 0, 128, t)
                            pp.done()
                        pk.done()
                    rkd = misc(5 + (h % 2), 68 + 36 * (h % 2), 36)
                    act(rkd, V(pssd.t[:, 0:36], pssd.r), AF.Ln, bias=V(EPSC[:, 2:3], EPSCr))
                    pssd.done()
                    act(rkd, rkd, AF.Exp, scale=-0.5)
                    rkbase = 68 + 36 * (h % 2)
                    for k0 in range(0, 18, 4):
                        nk = min(4, 18 - k0)
                        pvb = PS.get()
                        pv3 = pvb.t[:, 0:nk * 128].rearrange("p (k e) -> p k e", e=128)
                        for i in range(nk):
                            kt = k0 + i
                            for kc in range(8):
                                mm(V(pv3[:, i, :], pvb.r), slot_cols(Hs[kc], kt * 128, 128), wdv.v(vdv[:, kc, :]), kc == 0, kc == 7)
                        res = [ARres[dv_s][t] for t, (a, m_) in enumerate(TT) if a < (k0 + nk) * 128 and k0 * 128 < a + m_]
                        vcopy(V(dvv[:, k0:k0 + nk, :], res), V(pv3, pvb.r))
                        pvb.done()
                    def diff_prep(t, qi):
                        a, n = TT[t]
                        pq = PS.get()
                        pqv = V(pq.t[:, 0:n], pq.r)
                        for kc in range(8):
                            mm(pqv, wdq.v(vdq[:, kc, :]), hv(kc, t), kc == 0, kc == 7)
                        if t > 0:
                            pp = PS.get()
                            ppv = V(pp.t[:, 0:n], pp.r)
                            for kc in range(8):
                                mm(ppv, wdq.v(vdqP[:, kc, :]), hv(kc, t), kc == 0, kc == 7)
                        sq = BP.get()
                        act(V(sq.t[:, 0:n], sq.r), pqv, AF.Square)
                        pss = PS.get()
                        pssv = V(pss.t[:, 0:n], pss.r)
                        mm(pssv, V(BLKONES[:, 0:128], CONSTr), V(sq.t[:, 0:n], sq.r), True, True)
                        sq.done()
                        rs = rstd_from(pssv, n, 128, 1.0 / 64, 0)
                        pss.done()
                        (qa_t, qa_r), (qb_t, qb_r) = QD[qi % 2]
                        if t == 0:
                            stt(V(qa_t[0:64, 0:n], qa_r), V(pq.t[0:64, 0:n], pq.r), smcol(l, 107, 0, 64), V(rs.t[0:64, 0:n], rs.r), ALU.mult, ALU.mult)
                            stt(V(qb_t[64:128, 0:n], qb_r), V(pq.t[64:128, 0:n], pq.r), smcol(l, 107, 64, 128), V(rs.t[64:128, 0:n], rs.r), ALU.mult, ALU.mult)
                        else:
                            rope([(V(qa_t[0:64, 0:n], qa_r), 0, 64), (V(qb_t[64:128, 0:n], qb_r), 64, 128)], pqv, ppv,
                                 smcol(l, 107), smcol(l, 108), 192, 0, 128, t, final=(rs.t, rs.r))
                            pp.done()
                        pq.done()
                        rs.done()
                        return (V(qa_t[:, 0:n], qa_r), V(qb_t[:, 0:n], qb_r))

                    def diff_attend(t, qpair):
                        a, n = TT[t]
                        kts = list(range(18)) if t > 0 else [0, 1]
                        ocomp = []
                        for m in range(2):
                            dqv = qpair[m]
                            accO, accR = PS.get(), PS.get()
                            aov, arv = V(accO.t[:, 0:n], accO.r), V(accR.t[:, 0:n], accR.r)
                            pend = []
                            for i in range(len(kts) + LAG):
                                if i < len(kts):
                                    kt = kts[i]
                                    sps = PS.get()
                                    sv = V(sps.t[:, 0:n], sps.r)
                                    mm(sv, slot_cols(dk_s, kt * 128, 128), dqv, True, True)
                                    pt = BP.get()
                                    act(V(pt.t[:, 0:n], pt.r), sv, AF.Exp,
                                        scale=V(MISC[:, rkbase + 2 * kt + m:rkbase + 2 * kt + m + 1], MISCr[5 + (h % 2)]))
                                    sps.done()
                                    pend.append((kt, pt))
                                if i >= LAG:
                                    kt, pt = pend.pop(0)
                                    res = [ARres[dv_s][tt2] for tt2, (a2, m2) in enumerate(TT) if a2 <= kt * 128 < a2 + m2]
                                    mm(aov, V(dvv[:, kt, :], res), V(pt.t[:, 0:n], pt.r), kt == kts[0], kt == kts[-1])
                                    mm(arv, V(ONES[:, 0:128], ONESr), V(pt.t[:, 0:n], pt.r), kt == kts[0], kt == kts[-1])
                                    pt.done()
                            rc = FP.get()
                            recip(V(rc.t[:, 0:n], rc.r), arv)
                            accR.done()
                            oc = FP.get()
                            tt_(V(oc.t[:, 0:n], oc.r), aov, V(rc.t[:, 0:n], rc.r), ALU.mult)
                            accO.done()
                            rc.done()
                            ocomp.append(oc)
                        o0, o1 = ocomp
                        ov = V(o0.t[:, 0:n], o0.r)
                        stt(ov, V(o1.t[:, 0:n], o1.r), neglam, ov, ALU.mult, ALU.add)
                        o1.done()
                        sq = BP.get()
                        act(V(sq.t[:, 0:n], sq.r), ov, AF.Square)
                        pss = PS.get()
                        pssv = V(pss.t[:, 0:n], pss.r)
                        mm(pssv, V(ONES[:, 0:128], ONESr), V(sq.t[:, 0:n], sq.r), True, True)
                        sq.done()
                        rs = rstd_from(pssv, n, 128, 1.0 / 128, 0)
                        pss.done()
                        stt(slot_v(ydiff[h], t), ov, gsub, V(rs.t[:, 0:n], rs.r), ALU.mult, ALU.mult)
                        rs.done()
                        o0.done()

                    qcur = diff_prep(qtiles[0], qcount)
                    for ti, t in enumerate(qtiles):
                        qcount += 1
                        qnext = diff_prep(qtiles[ti + 1], qcount) if ti + 1 < len(qtiles) else None
                        diff_attend(t, qcur)
                        qcur = qnext
                    w_free(wdq)
                    w_free(wdk)
                    w_free(wdv)
                if stop_after == (l, 'diff'):
                    raise _Stop()
                afree([dk_s, dv_s])

                mtiles = qtiles
                wbr = w_br_d[l]
                wo = w_o_d[l]

                def merge(branches, ys, tail=None):
                    Ms = aalloc(2)
                    for p_ in range(4):
                        for di in range(2):
                            dd = 2 * p_ + di
                            wg = w_get([(i * 1024, 8, 128, win[:, :, 3744 + br * 1024 + dd * 128:3744 + br * 1024 + (dd + 1) * 128])
                                        for i, br in enumerate(branches)], cap=3)
                            wb = w_get([(i * 512, 4, 128, wbr[br][:, dd * 128:(dd + 1) * 128].rearrange("(c p) n -> p c n", p=128))
                                        for i, br in enumerate(branches)], cap=3)
                            for t in mtiles:
                                a, n = TT[t]
                                terms = []
                                for i, br in enumerate(branches):
                                    vg, vb = wg.view(8, 128, i * 1024), wb.view(4, 128, i * 512)
                                    pg, pz = PS.get(), PS.get()
                                    pgv, pzv = V(pg.t[:, 0:n], pg.r), V(pz.t[:, 0:n], pz.r)
                                    for kc in range(8):
                                        mm(pgv, wg.v(vg[:, kc, :]), hv(kc, t), kc == 0, kc == 7)
                                    for c in range(4):
                                        mm(pzv, wb.v(vb[:, c, :]), slot_v(ys[i][c], t), c == 0, c == 3)
                                    sg = FP.get()
                                    sgv = V(sg.t[:, 0:n], sg.r)
                                    act(sgv, pgv, AF.Sigmoid)
                                    pg.done()
                                    if len(branches) == 1:
                                        tt_(slot_v(Ms[di], t), sgv, pzv, ALU.mult)
                                        sg.done()
                                    else:
                                        tt_(sgv, sgv, pzv, ALU.mult)
                                        terms.append(sg)
                                    pz.done()
                                if len(branches) == 2:
                                    tt_(slot_v(Ms[di], t), V(terms[0].t[:, 0:n], terms[0].r), V(terms[1].t[:, 0:n], terms[1].r), ALU.add)
                                    terms[0].done()
                                    terms[1].done()
                            w_free(wg)
                            w_free(wb)
                        wop = w_get([(0, 2, 1024, wo[2 * p_ * 128:(2 * p_ + 2) * 128, :].rearrange("(c p) n -> p c n", p=128))], cap=3)
                        vo = wop.view(2, 1024)
                        for t in mtiles:
                            a, n = TT[t]
                            w = 1 if t == 0 else 0
                            for do in range(8):
                                po = PS.get()
                                pov = V(po.t[:, 0:n], po.r)
                                for di in range(2):
                                    mm(pov, wop.v(vo[:, di, do * 128:(do + 1) * 128]), slot_v(Ms[di], t), di == 0, di == 1)
                                stt(xv(do, t), pov, modcol(l, 5, do, w), xv(do, t), ALU.mult, ALU.add)
                                po.done()
                            if tail is not None and p_ == 3:
                                ti = mtiles.index(t)
                                if ti > 0:
                                    tail(mtiles[ti - 1])
                        if tail is not None and p_ == 3:
                            tail(mtiles[-1])
                        w_free(wop)
                    afree(Ms)

                merge([0, 1], [ymla, ydiff])
                if stop_after == (l, 'merge1'):
                    raise _Stop()
                afree(ymla + ydiff)

                yconv = aalloc(4)
                u_s = aalloc(1)[0]
                for c in range(4):
                    wcx = w_get([(0, 8, 128, win[:, :, 2720 + c * 128:2720 + (c + 1) * 128]),
                                 (1024, 8, 128, win[:, :, 3232 + c * 128:3232 + (c + 1) * 128])], cap=3)
                    wcb = w_get([(0, 8, 128, win[:, :, 2208 + c * 128:2208 + (c + 1) * 128])], cap=3)
                    vcc, vcx, vcb = wcx.view(8, 128), wcx.view(8, 128, 1024), wcb.view(8, 128)
                    for t in alltiles:
                        a, n = TT[t]
                        pc, px = PS.get(), PS.get()
                        pcv, pxv = V(pc.t[:, 0:n], pc.r), V(px.t[:, 0:n], px.r)
                        for kc in range(8):
                            mm(pcv, wcx.v(vcc[:, kc, :]), hv(kc, t), kc == 0, kc == 7)
                        for kc in range(8):
                            mm(pxv, wcx.v(vcx[:, kc, :]), hv(kc, t), kc == 0, kc == 7)
                        sc = FP.get()
                        act(V(sc.t[:, 0:n], sc.r), pcv, AF.Copy)
                        pc.done()
                        tt_(slot_v(u_s, t), pxv, V(sc.t[:, 0:n], sc.r), ALU.mult)
                        px.done()
                        sc.done()
                    for t in mtiles:
                        a, n = TT[t]
                        seg_s, seg_e = (0, NCTX) if t == 0 else (NCTX, NTOK)
                        ac = FP.get()
                        acv = V(ac.t[:, 0:n], ac.r)
                        cw = lambda j: smcol(l, 112 + c * 3 + j)
                        ts(acv, slot_v(u_s, t), cw(1), 0.0, ALU.mult, ALU.add)
                        lo = max(a, seg_s + 1)
                        stt(V(ac.t[:, lo - a:n], ac.r), slot_cols(u_s, lo - 1, a + n - lo), cw(0), V(ac.t[:, lo - a:n], ac.r), ALU.mult, ALU.add)
                        hi = min(a + n, seg_e - 1)
                        stt(V(ac.t[:, 0:hi - a], ac.r), slot_cols(u_s, a + 1, hi - a), cw(2), V(ac.t[:, 0:hi - a], ac.r), ALU.mult, ALU.add)
                        pb_ = PS.get()
                        pbv = V(pb_.t[:, 0:n], pb_.r)
                        for kc in range(8):
                            mm(pbv, wcb.v(vcb[:, kc, :]), hv(kc, t), kc == 0, kc == 7)
                        tt_(slot_v(yconv[c], t), pbv, acv, ALU.mult)
                        pb_.done()
                        ac.done()
                    w_free(wcx)
                    w_free(wcb)
                afree([u_s])
                merge([2], [yconv], tail=(lambda t_: compute_H(l, 2, [t_])) if ffn2_follows else None)
                afree(yconv)
                WS.cur_cap = 12

            def network():
                mod_begin(0)
                mod_pieces(0, 36)
                mod_finish(0)
                for l in range(n_layers):
                    last = (l == DEPTH - 1)
                    derive(l)
                    WS.cur_cap = 12
                    stop1 = stop_after == (l, "ffn1")
                    ffn(l, 0, f1u_d, f1d_d, [0, 1, 2, 3, 4], tail=None if stop1 else (lambda t_, l=l: compute_H(l, 1, [t_])))
                    if stop1:
                        break
                    mixer(l, last, ffn2_follows=(stop_after != (l, "mix")))
                    if stop_after == (l, "mix"):
                        break
                    nxt = l + 1 < n_layers
                    if nxt:
                        mod_begin(l + 1)
                    ffn(l, 2, f2u_d, f2d_d, [1, 2, 3, 4] if last else [0, 1, 2, 3, 4], h_ready=True,
                        interleave=(lambda g, l=l: mod_pieces(l + 1, 4)) if nxt else None)
                    if nxt:
                        mod_pieces(l + 1, 36)
                        mod_finish(l + 1)

            try:
                network()
            except _Stop:
                pass
            outr = Res("out")
            for c in range(8):
                S.dma("sp", [(outT_d[c * 128:(c + 1) * 128, :], XT[:, c, NCTX:NTOK])], outr, reads=[V(None, Xres[c][1:5])])
                if debug_ctx:
                    S.dma("sp", [(outC_d[c * 128:(c + 1) * 128, :], XT[:, c, 0:NCTX])], outr, reads=[V(None, Xres[c][0])])
            S.wait_all("sp", [V(None, [Xres[c][t] for c in range(8) for t in range(5)])])
            if stop_after is None or stop_after[1] in ("ffn1", "mix"):
                afree(Hs)

        S.dry = True
        program()
        if stop_after is None:
            assert len(arena_free) == NSLOT and not PS.live and not FP.live and not BP.live, (len(arena_free), PS.live, FP.live, BP.live)
        arena_free[:] = list(range(NSLOT))
        PS.free[:] = list(range(8))
        FP.free[:] = list(range(len(FP.tiles)))
        BP.free[:] = list(range(len(BP.tiles)))
        S.dry = False
        program()
        if stop_after is None:
            assert WS.nxt == len(WS.descs) and WS.outstanding == 0
        build_program.stats = dict(ninst=S.ninst, nwait=S.nwait, npieces=len(WS.descs),
                                   nsig={k: e.nsig for k, e in S.E.items()})
    return nc


def _perm(R):
    q = R // 4
    idx = np.arange(R).reshape(2, 2, q)
    return idx[:, ::-1, :].reshape(R)


def _tables():
    tab = np.zeros((128, NTAB), np.float32)

    def fill(base, R, rows0):
        q = R // 4
        invf = (10000.0 ** (-np.arange(q, dtype=np.float32) / q)).astype(np.float32)
        rows = np.arange(32, dtype=np.float32)
        cols = np.arange(64, dtype=np.float32)
        for d in range(R):
            a, h, f = d // (2 * q), (d // q) % 2, d % q
            sign = -1.0 if h == 0 else 1.0
            for r0_ in rows0:
                p = r0_ + d
                if a == 0:
                    ang = (rows * invf[f]).astype(np.float32)
                    tab[p, base:base + 32] = np.cos(ang)
                    tab[p, base + 32:base + 96] = 1.0
                    tab[p, base + 96:base + 128] = sign * np.sin(ang)
                    tab[p, base + 128:base + 192] = 1.0
                else:
                    ang = (cols * invf[f]).astype(np.float32)
                    tab[p, base:base + 32] = 1.0
                    tab[p, base + 32:base + 96] = np.cos(ang)
                    tab[p, base + 96:base + 128] = 1.0
                    tab[p, base + 128:base + 192] = sign * np.sin(ang)
    fill(0, 32, [64])
    fill(192, 64, [0, 64])
    return tab


def _host_prep(inputs):
    f = lambda k: np.ascontiguousarray(np.asarray(inputs[k], dtype=np.float32))
    P32, P64 = _perm(32), _perm(64)
    w_in = f("w_in")
    w_uq = f("w_uq")
    sm = np.zeros((128, DEPTH, NS), np.float32)
    b_mod, norm_g = f("b_mod"), f("norm_g")
    g_cq, g_ckv = f("g_cq"), f("g_ckv")
    gq, gk, gqd, gkd = f("g_q_mla"), f("g_k_mla"), f("g_q_diff"), f("g_k_diff")
    gsub, convw, lam = f("g_subln"), f("conv_w"), f("lam")
    for l in range(DEPTH):
        sm[:, l, 0:72] = b_mod[l].reshape(72, 128).T
        sm[:, l, 72:96] = norm_g[l].reshape(24, 128).T
        sm[:, l, 96:99] = g_cq[l].reshape(3, 128).T
        sm[:, l, 99:101] = g_ckv[l].reshape(2, 128).T
        sm[0:96, l, 101] = gq[l, 0:96]
        sm[64:96, l, 103] = gq[l, 64:96][P32]
        sm[0:96, l, 104] = gk[l, 0:96]
        sm[64:96, l, 106] = gk[l, 64:96][P32]
        for r0_ in (0, 64):
            sm[r0_:r0_ + 64, l, 107] = gqd[l]
            sm[r0_:r0_ + 64, l, 108] = gqd[l][P64]
            sm[r0_:r0_ + 64, l, 109] = gkd[l]
            sm[r0_:r0_ + 64, l, 110] = gkd[l][P64]
        sm[:, l, 111] = gsub[l]
        sm[:, l, 112:124] = convw[l].reshape(3, 4, 128).transpose(2, 1, 0).reshape(128, 12)
        sm[0:64, l, 124:128] = lam[l].T
    shared = {
        "smalls": np.ascontiguousarray(sm.reshape(128, DEPTH * NS)),
        "tabs": _tables(),
        "w_mod": f("w_mod"), "ffn1_up": f("ffn1_up"), "ffn1_down": f("ffn1_down"),
        "ffn2_up": f("ffn2_up"), "ffn2_down": f("ffn2_down"), "w_in": w_in,
        "w_krP": np.ascontiguousarray(np.concatenate([w_in[:, :, 576:640], w_in[:, :, 640:672][:, :, P32]], axis=2)),
        "w_dqP": np.ascontiguousarray(w_in[:, :, 672:1184].reshape(DEPTH, D, 8, 64)[:, :, :, P64].reshape(DEPTH, D, 512)),
        "w_dkP": np.ascontiguousarray(w_in[:, :, 1184:1696].reshape(DEPTH, D, 8, 64)[:, :, :, P64].reshape(DEPTH, D, 512)),
        "w_uq": w_uq,
        "w_uqP": np.ascontiguousarray(np.concatenate([w_uq.reshape(DEPTH, 384, 8, 96)[:, :, :, 0:64],
                                                      w_uq.reshape(DEPTH, 384, 8, 96)[:, :, :, 64:96][:, :, :, P32]], axis=3).reshape(DEPTH, 384, 768)),
        "w_ukv": f("w_ukv"), "w_br": f("w_br"), "w_o": f("w_o"),
    }
    x, c, ctx, c_ctx = f("x"), f("c"), f("ctx"), f("c_ctx")
    in_maps = []
    for b in range(8):
        cc = np.zeros((128, 8, 2), np.float32)
        cc[:, :, 0] = c[b].reshape(8, 128).T
        cc[:, :, 1] = c_ctx.reshape(8, 128).T
        m = dict(shared)
        m["xT"] = np.ascontiguousarray(x[b].T)
        m["ctxT"] = np.ascontiguousarray(ctx[b].T)
        m["cc"] = np.ascontiguousarray(cc.reshape(128, 16))
        in_maps.append(m)
    return in_maps


_NC_CACHE = {}


def kernel(**inputs):
    in_maps = _host_prep(inputs)
    if "nc" not in _NC_CACHE:
        _NC_CACHE["nc"] = build_program()
    nc = _NC_CACHE["nc"]
    res = run_bass_kernel_spmd(nc, in_maps, core_ids=list(range(8)))
    out = np.stack([np.asarray(res.results[b]["outT"]).T for b in range(8)], axis=0)
    return np.ascontiguousarray(out.astype(np.float32))
```

```python
import math
import contextlib
import numpy as np
import concourse.bass as bass
import concourse.mybir as mybir
from concourse.bass_utils import run_bass_kernel_spmd

F32 = mybir.dt.float32
BF16 = mybir.dt.bfloat16
AF = mybir.ActivationFunctionType
ALU = mybir.AluOpType

DEPTH = 4
D = 1024
NCTX = 256
NLAT = 2048
NTOK = NCTX + NLAT
DFF = 2816
EPS = 1e-6
TT = [(0, 256), (256, 512), (768, 512), (1280, 512), (1792, 512)]
NSLOT = 21
SLOTW = 2304
NS = 128
NTAB = 384
SEM_CHUNK = 12000


class Res:
    __slots__ = ("name", "w", "r", "dsem", "dcnt", "excl")

    def __init__(self, name, excl=False):
        self.name = name
        self.w = None
        self.r = {}
        self.dsem = None
        self.dcnt = 0
        self.excl = excl


class V:
    __slots__ = ("ap", "res")

    def __init__(self, ap, res):
        self.ap = ap
        self.res = list(res) if isinstance(res, (list, tuple)) else [res]


class Eng:
    def __init__(self, name, h):
        self.name = name
        self.h = h
        self.sems = []
        self.nsig = 0
        self.waited = {}


class Sched:
    def __init__(self, nc, es):
        self.nc = nc
        self.es = es
        self.dry = False
        self.E = {"pe": Eng("pe", nc.tensor), "act": Eng("act", nc.scalar), "dve": Eng("dve", nc.vector),
                  "pool": Eng("pool", nc.gpsimd), "sp": Eng("sp", nc.sync)}
        self.ninst = 0
        self.nwait = 0

    def new_sem(self, name):
        return self.es.enter_context(self.nc.semaphore(name))

    def _sem_for(self, e, idx):
        ci = (idx - 1) // SEM_CHUNK
        while len(e.sems) <= ci:
            e.sems.append(self.new_sem(f"s_{e.name}_{len(e.sems)}"))
        return e.sems[ci], (idx - 1) % SEM_CHUNK + 1

    def _wait(self, e, deps, embed=False):
        need = []
        for ev in deps:
            if ev is None:
                continue
            if ev[0] == "E":
                _, name, idx = ev
                src = self.E[name]
                if name == e.name and name == "pe":
                    continue
                assert idx <= src.nsig, f"wait on pending signal {ev} by {e.name}"
                if e.waited.get(name, 0) >= idx:
                    continue
                e.waited[name] = idx
                need = [x for x in need if x[0] != name]
                need.append((name,) + self._sem_for(src, idx))
            else:
                _, owner, val = ev
                key = ("D", id(owner))
                if e.waited.get(key, 0) >= val:
                    continue
                e.waited[key] = val
                need = [x for x in need if x[0] != key]
                need.append((key, owner.dsem, val))
        last = None
        if embed and need:
            last = need.pop()
        for (_, sem, val) in need:
            e.h.wait_ge(sem, val)
            self.nwait += 1
        return last

    def op(self, eng, fn, reads, writes, sig=True):
        if self.dry:
            return
        rres = [r for v in reads if v is not None for r in v.res]
        wres = [r for v in writes if v is not None for r in v.res]
        e = self.E[eng]
        deps = []
        for r in rres:
            deps.append(r.w)
            if r.excl:
                deps.extend(v for k, v in r.r.items() if k != e.name)
        for w in wres:
            deps.append(w.w)
            deps.extend(w.r.values())
        emb = self._wait(e, deps, embed=True)
        inst = fn()
        if emb is not None:
            inst._wait_ge(emb[1], emb[2])
        self.ninst += 1
        if sig:
            e.nsig += 1
            sem, _ = self._sem_for(e, e.nsig)
            inst.then_inc(sem, 1)
            ev = ("E", e.name, e.nsig)
        else:
            ev = ("E", e.name, e.nsig + 1)
        for r in rres:
            r.r[e.name] = ev
        for w in wres:
            w.w = ev
            w.r = {}

    def dma(self, q, pairs, owner, reads=(), writes=()):
        if self.dry:
            return
        rres = [r for v in reads for r in v.res]
        wres = [r for v in writes for r in v.res]
        e = self.E[q]
        if owner.dsem is None:
            owner.dsem = self.new_sem(f"d_{owner.name}")
        deps = []
        for r in rres:
            deps.append(r.w)
        for w in wres:
            deps.append(w.w)
            deps.extend(w.r.values())
        emb = self._wait(e, deps, embed=True)
        for (o, i) in pairs:
            di = e.h.dma_start(out=o, in_=i)
            if emb is not None:
                di._wait_ge(emb[1], emb[2])
                emb = None
            di.then_inc(owner.dsem, 16)
            owner.dcnt += 16
            self.ninst += 1
        ev = ("D", owner, owner.dcnt)
        for r in rres:
            r.r[("D", id(owner))] = ev
        for w in wres:
            w.w = ev
            w.r = {}

    def wait_all(self, eng, views):
        if self.dry:
            return
        e = self.E[eng]
        deps = []
        for v in views:
            for r in v.res:
                deps.append(r.w)
                deps.extend(r.r.values())
        self._wait(e, deps)


class TilePool:
    def __init__(self, name, tiles):
        self.name = name
        self.tiles = tiles
        self.free = list(range(len(tiles)))
        self.live = set()

    def get(self):
        assert self.free, f"pool {self.name} exhausted"
        i = self.free.pop(0)
        self.live.add(i)
        t, r = self.tiles[i]
        return Tile(self, i, t, r)


class _Stop(Exception):
    pass


class Tile:
    __slots__ = ("pool", "i", "t", "r")

    def __init__(self, pool, i, t, r):
        self.pool, self.i, self.t, self.r = pool, i, t, r

    def v(self, ap=None):
        return V(self.t[:] if ap is None else ap, self.r)

    def done(self):
        self.pool.live.discard(self.i)
        self.pool.free.append(self.i)


def build_program(n_layers=DEPTH, debug_ctx=False, stop_after=None):
    nc = bass.Bass("TRN2", target_bir_lowering=False)
    din = {}

    def inp(name, shape):
        din[name] = nc.dram_tensor(name, list(shape), F32, kind="ExternalInput").ap()
        return din[name]

    xT_d = inp("xT", [D, NLAT])
    ctxT_d = inp("ctxT", [D, NCTX])
    cc_d = inp("cc", [128, 16])
    smalls_d = inp("smalls", [128, DEPTH * NS])
    tabs_d = inp("tabs", [128, NTAB])
    w_mod_d = inp("w_mod", [DEPTH, D, 9 * D])
    f1u_d = inp("ffn1_up", [DEPTH, D, 2 * DFF])
    f1d_d = inp("ffn1_down", [DEPTH, DFF, D])
    f2u_d = inp("ffn2_up", [DEPTH, D, 2 * DFF])
    f2d_d = inp("ffn2_down", [DEPTH, DFF, D])
    w_in_d = inp("w_in", [DEPTH, D, 6816])
    w_krP_d = inp("w_krP", [DEPTH, D, 96])
    w_dqP_d = inp("w_dqP", [DEPTH, D, 512])
    w_dkP_d = inp("w_dkP", [DEPTH, D, 512])
    w_uq_d = inp("w_uq", [DEPTH, 384, 768])
    w_uqP_d = inp("w_uqP", [DEPTH, 384, 768])
    w_ukv_d = inp("w_ukv", [DEPTH, 256, 1024])
    w_br_d = inp("w_br", [DEPTH, 3, 512, D])
    w_o_d = inp("w_o", [DEPTH, D, D])
    outT_d = nc.dram_tensor("outT", [D, NLAT], F32, kind="ExternalOutput").ap()
    outC_d = nc.dram_tensor("outC", [D, NCTX], F32, kind="ExternalOutput").ap() if debug_ctx else None

    es = contextlib.ExitStack()
    with es:
        S = Sched(nc, es)

        def sbuf(name, shape, dt):
            return es.enter_context(nc.sbuf_tensor(name, list(shape), dt))

        XT = sbuf("XT", [128, 8, NTOK], F32)
        Xres = [[Res(f"X{c}_{t}") for t in range(5)] for c in range(8)]
        AR = sbuf("AR", [128, NSLOT, SLOTW], BF16)
        ARres = [[Res(f"A{s}_{t}") for t in range(5)] for s in range(NSLOT)]
        SM = sbuf("SM", [128, DEPTH * NS], F32)
        SMr = Res("SM")
        TAB = sbuf("TAB", [128, NTAB], F32)
        TABr = Res("TAB")
        CCf = sbuf("CCf", [128, 16], F32)
        CCfr = Res("CCf")
        COND = sbuf("COND", [128, 16], BF16)
        CONDr = Res("COND")
        ONES = sbuf("ONES", [128, 128], BF16)
        ONESr = Res("ONES")
        ONESF = sbuf("ONESF", [128, 128], F32)
        ONESFr = Res("ONESF")
        EPSC = sbuf("EPSC", [128, 4], F32)
        EPSCr = Res("EPSC")
        MODT = [sbuf(f"MODT{i}", [128, 144], F32) for i in range(2)]
        MODTr = [Res(f"MODT{i}") for i in range(2)]
        DER = sbuf("DER", [128, 96], F32)
        DERr = Res("DER")
        MISC = sbuf("MISC", [128, 160], F32)
        MISCr = [Res(f"MISC{i}") for i in range(8)]

        BLK2 = sbuf("BLK2", [128, 2], BF16)
        BLKONES = sbuf("BLKONES", [128, 128], BF16)
        MASKR = sbuf("MASKR", [128, 1], BF16)
        CONSTr = Res("CONST")
        QM = [(sbuf(f"QM{i}", [128, 512], BF16), Res(f"QM{i}")) for i in range(2)]
        QD = [((sbuf(f"QA{i}", [128, 512], BF16), Res(f"QA{i}")), (sbuf(f"QB{i}", [128, 512], BF16), Res(f"QB{i}"))) for i in range(2)]
        ps_tiles = []
        for i in range(8):
            t = es.enter_context(nc.psum_tensor(f"ps{i}", [128, 512], F32))
            ps_tiles.append((t, Res(f"ps{i}", excl=True)))
        PS = TilePool("psum", ps_tiles)
        FP = TilePool("F", [(sbuf(f"F{i}", [128, 512], F32), Res(f"F{i}")) for i in range(6)])
        BP = TilePool("B", [(sbuf(f"B{i}", [128, 512], BF16), Res(f"B{i}")) for i in range(7)])

        arena_free = list(range(NSLOT))

        def aalloc(n=1):
            assert len(arena_free) >= n, f"arena exhausted (need {n}, have {len(arena_free)})"
            out = [arena_free.pop(0) for _ in range(n)]
            return out

        def afree(slots):
            for s in slots:
                arena_free.append(s)

        def slot_v(s, t, p0=0, p1=128):
            a, n = TT[t]
            return V(AR[p0:p1, s, a:a + n], ARres[s][t])

        def slot_cols(s, c0, n, p0=0, p1=128):
            res = [ARres[s][t] for t, (a, m) in enumerate(TT) if a < c0 + n and c0 < a + m]
            return V(AR[p0:p1, s, c0:c0 + n], res)

        def slot_all(s):
            return V(None, ARres[s])

        def xv(c, t):
            a, n = TT[t]
            return V(XT[:, c, a:a + n], Xres[c][t])

        class WS:
            descs = []
            nxt = 0
            outstanding = 0
            cur_cap = 8
            req = 0

        class Piece:
            __slots__ = ("slot", "idx")

            def view(self, k, n, off=0):
                return AR[:, self.slot, off:off + k * n].rearrange("p (k n) -> p k n", n=n)

            def v(self, ap):
                return V(ap, ARres[self.slot])

        def w_issue():
            while WS.nxt < len(WS.descs):
                pairs, cap = WS.descs[WS.nxt][0], WS.descs[WS.nxt][1]
                if WS.outstanding + 1 > min(cap, WS.cur_cap) or not arena_free:
                    break
                s = arena_free.pop(0)
                dp = []
                for (off, k, n, src) in pairs:
                    dp.append((AR[:, s, off:off + k * n].rearrange("p (k n) -> p k n", n=n), src))
                S.dma("pool", dp, ARres[s][0], writes=[slot_all(s)])
                WS.descs[WS.nxt] = (pairs, cap, s)
                WS.nxt += 1
                WS.outstanding += 1

        def w_get(pairs, cap=6):
            p = Piece()
            p.idx = WS.req
            WS.req += 1
            if S.dry:
                WS.descs.append((pairs, cap))
                p.slot = 0
                return p
            w_issue()
            assert p.idx < WS.nxt, f"weight piece {p.idx} could not be issued (outstanding={WS.outstanding}, free={len(arena_free)})"
            p.slot = WS.descs[p.idx][2]
            return p

        def w_free(p):
            if S.dry:
                return
            arena_free.append(p.slot)
            WS.outstanding -= 1
            w_issue()

        def mm(out, lhsT, rhs, start, stop, sig=None):
            S.op("pe", lambda: nc.tensor.matmul(out.ap, lhsT=lhsT.ap, rhs=rhs.ap, start=start, stop=stop),
                 [lhsT, rhs], [out], sig=(stop if sig is None else sig))

        def _a(x):
            return x.ap if isinstance(x, V) else x

        def _v(x):
            return x if isinstance(x, V) else None

        def act(out, in_, func, bias=None, scale=1.0):
            kw = {}
            if bias is not None:
                kw["bias"] = _a(bias)
            S.op("act", lambda: nc.scalar.activation(out=out.ap, in_=in_.ap, func=func, scale=_a(scale), **kw),
                 [in_, _v(bias), _v(scale)], [out])

        def tt_(out, in0, in1, op, eng="dve"):
            h = nc.vector if eng == "dve" else nc.gpsimd
            S.op(eng, lambda: h.tensor_tensor(out=out.ap, in0=in0.ap, in1=in1.ap, op=op), [in0, in1], [out])

        def stt(out, in0, scalar, in1, op0, op1, eng="dve"):
            h = nc.vector if eng == "dve" else nc.gpsimd
            S.op(eng, lambda: h.scalar_tensor_tensor(out=out.ap, in0=in0.ap, scalar=_a(scalar), in1=in1.ap, op0=op0, op1=op1),
                 [in0, _v(scalar), in1], [out])

        def ts(out, in0, s1, s2, op0, op1, eng="dve"):
            h = nc.vector if eng == "dve" else nc.gpsimd
            S.op(eng, lambda: h.tensor_scalar(out=out.ap, in0=in0.ap, scalar1=_a(s1), scalar2=_a(s2), op0=op0, op1=op1),
                 [in0, _v(s1), _v(s2)], [out])

        def recip(out, in_):
            S.op("dve", lambda: nc.vector.reciprocal(out=out.ap, in_=in_.ap), [in_], [out])

        def vcopy(out, in_):
            S.op("dve", lambda: nc.vector.tensor_copy(out=out.ap, in_=in_.ap), [in_], [out])

        def memset(t_ap, res, val):
            S.op("pool", lambda: nc.gpsimd.memset(t_ap, val), [], [V(None, res)])

        def smcol(l, c, p0=0, p1=128, n=1):
            return V(SM[p0:p1, l * NS + c:l * NS + c + n], SMr)

        def misc(i, c, n=1, p0=0, p1=128):
            return V(MISC[p0:p1, c:c + n], MISCr[i])

        def rstd_from(pss, n, P, scale, epscol, p0=0):
            rs = FP.get()
            rv = V(rs.t[p0:P, 0:n], rs.r)
            act(rv, pss, AF.Ln, bias=V(EPSC[p0:P, epscol:epscol + 1], EPSCr), scale=scale)
            act(rv, rv, AF.Exp, scale=-0.5)
            return rs

        def arecip(out, in_):
            act(out, in_, AF.Ln)
            act(out, out, AF.Exp, scale=-1.0)


        def program():
            WS.req = 0
            S.dma("sp", [(SM[:], smalls_d[:, :])], SMr, writes=[V(None, SMr)])
            S.dma("sp", [(TAB[:], tabs_d[:, :])], TABr, writes=[V(None, TABr)])
            S.dma("sp", [(CCf[:], cc_d[:, :])], CCfr, writes=[V(None, CCfr)])
            memset(ONES[:], ONESr, 1.0)
            memset(ONESF[:], ONESFr, 1.0)
            memset(EPSC[:, 0:1], EPSCr, EPS)
            memset(EPSC[:, 1:2], EPSCr, 96 * EPS)
            memset(EPSC[:, 2:3], EPSCr, 64 * EPS)
            memset(EPSC[:, 3:4], EPSCr, 0.0)
            memset(BLK2[:], CONSTr, 0.0)
            memset(BLK2[0:64, 0:1], CONSTr, 1.0)
            memset(BLK2[64:128, 1:2], CONSTr, 1.0)
            memset(BLKONES[:], CONSTr, 0.0)
            memset(BLKONES[0:64, 0:64], CONSTr, 1.0)
            memset(BLKONES[64:128, 64:128], CONSTr, 1.0)
            memset(MASKR[:], CONSTr, 0.0)
            memset(MASKR[64:96, :], CONSTr, 1.0)
            for (qt__, qr__) in QM + [x for pr_ in QD for x in pr_]:
                memset(qt__[:], qr__, 0.0)
            for c in range(8):
                S.dma("sp", [(XT[:, c, NCTX:NTOK], xT_d[c * 128:(c + 1) * 128, :])], Xres[c][1],
                      writes=[V(None, Xres[c][1:5])])
                S.dma("sp", [(XT[:, c, 0:NCTX], ctxT_d[c * 128:(c + 1) * 128, :])], Xres[c][0],
                      writes=[V(None, Xres[c][0])])
            act(V(COND[:], CONDr), V(CCf[:], CCfr), AF.Silu)

            Hs = aalloc(8)

            def hv(c, t):
                return slot_v(Hs[c], t)

            mod_state = {}

            def mod_begin(l):
                pm = PS.get()
                mod_state[l] = (pm, 0)

            def mod_pieces(l, n):
                pm, done = mod_state[l]
                pmv = pm.t[:, 0:144].rearrange("p (c w) -> p c w", w=2)
                wv = w_mod_d[l].rearrange("(kc p) n -> p kc n", p=128)
                for pi in range(done, min(36, done + n)):
                    pc = w_get([(0, 8, 256, wv[:, :, pi * 256:(pi + 1) * 256])], cap=10)
                    pvw = pc.view(8, 256)
                    for j in range(2):
                        col = 2 * pi + j
                        for kc in range(8):
                            mm(V(pmv[:, col, :], pm.r), pc.v(pvw[:, kc, j * 128:(j + 1) * 128]),
                               V(COND[:, 2 * kc:2 * kc + 2], CONDr), kc == 0, kc == 7)
                    w_free(pc)
                mod_state[l] = (pm, min(36, done + n))

            def mod_finish(l, c0=0, c1=72, release=True):
                pm, done = mod_state[l]
                assert done >= (c1 + 1) // 2
                mt, mr = MODT[l % 2], MODTr[l % 2]
                tt_(V(mt[:, 2 * c0:2 * c1].rearrange("p (c w) -> p c w", w=2), mr),
                    V(pm.t[:, 2 * c0:2 * c1].rearrange("p (c w) -> p c w", w=2), pm.r),
                    V(SM[:, l * NS + c0:l * NS + c1].unsqueeze(2).broadcast_to([128, c1 - c0, 2]), SMr), ALU.add)
                if release:
                    pm.done()

            def modcol(l, i, c, w):
                k = ((i * 8 + c) * 2) + w
                return V(MODT[l % 2][:, k:k + 1], MODTr[l % 2])

            def derive(l, js=(0, 1, 2)):
                mt, mr = MODT[l % 2], MODTr[l % 2]
                for j in js:
                    sc = V(mt[:, (3 * j + 1) * 16:(3 * j + 1) * 16 + 16].rearrange("p (c w) -> p c w", w=2), mr)
                    gn = V(SM[:, l * NS + 72 + j * 8:l * NS + 72 + j * 8 + 8].unsqueeze(2).broadcast_to([128, 8, 2]), SMr)
                    stt(V(DER[:, j * 16:(j + 1) * 16].rearrange("p (c w) -> p c w", w=2), DERr), sc, 1.0, gn, ALU.add, ALU.mult)
                for k, j in ((3, 0), (4, 2)):
                    if j in js:
                        g = V(mt[:, (3 * j + 2) * 16:(3 * j + 2) * 16 + 16], mr)
                        ts(V(DER[:, k * 16:(k + 1) * 16], DERr), g, 0.5, 0.0, ALU.mult, ALU.add)

            def gs_col(j, c, w):
                k = j * 16 + c * 2 + w
                return V(DER[:, k:k + 1], DERr)

            def hg_col(j, c, w):
                k = (3 if j == 0 else 4) * 16 + c * 2 + w
                return V(DER[:, k:k + 1], DERr)

            def compute_H(l, j, tiles):
                for t in tiles:
                    a, n = TT[t]
                    w = 1 if t == 0 else 0
                    pss = PS.get()
                    pv = V(pss.t[:, 0:n], pss.r)
                    for c in range(8):
                        sq = BP.get()
                        sv = V(sq.t[:, 0:n], sq.r)
                        act(sv, xv(c, t), AF.Square)
                        mm(pv, V(ONES[:, 0:128], ONESr), sv, c == 0, c == 7, sig=True)
                        sq.done()
                    rs = rstd_from(pv, n, 128, 1.0 / D, 0)
                    pss.done()
                    rv = V(rs.t[:, 0:n], rs.r)
                    for c in range(8):
                        tmp = FP.get()
                        tv = V(tmp.t[:, 0:n], tmp.r)
                        stt(tv, xv(c, t), gs_col(j, c, w), rv, ALU.mult, ALU.mult)
                        act(hv(c, t), tv, AF.Identity, bias=modcol(l, 3 * j, c, w))
                        tmp.done()
                    rs.done()

            def ffn(l, j, up_d, down_d, tiles, interleave=None, h_ready=False, tail=None):
                if not h_ready:
                    compute_H(l, j, tiles)
                upv = up_d[l].rearrange("(kc p) n -> p kc n", p=128)
                for g in range(11):
                    pa = w_get([(0, 8, 256, upv[:, :, g * 256:(g + 1) * 256])], cap=12)
                    pb = w_get([(0, 8, 256, upv[:, :, DFF + g * 256:DFF + (g + 1) * 256])], cap=12)
                    pd = w_get([(0, 2, 1024, down_d[l][g * 256:(g + 1) * 256, :].rearrange("(fc p) n -> p fc n", p=128))], cap=12)
                    va, vb, vd = pa.view(8, 256), pb.view(8, 256), pd.view(2, 1024)
                    def up_part(t):
                        a, n = TT[t]
                        gts = []
                        for i in range(2):
                            qa, qb = PS.get(), PS.get()
                            qav, qbv = V(qa.t[:, 0:n], qa.r), V(qb.t[:, 0:n], qb.r)
                            for kc in range(8):
                                mm(qav, pa.v(va[:, kc, i * 128:(i + 1) * 128]), hv(kc, t), kc == 0, kc == 7)
                            for kc in range(8):
                                mm(qbv, pb.v(vb[:, kc, i * 128:(i + 1) * 128]), hv(kc, t), kc == 0, kc == 7)
                            sa = FP.get()
                            sav = V(sa.t[:, 0:n], sa.r)
                            act(sav, qav, AF.Silu)
                            qa.done()
                            gt = BP.get()
                            tt_(V(gt.t[:, 0:n], gt.r), qbv, sav, ALU.mult)
                            qb.done()
                            sa.done()
                            gts.append(gt)
                        return gts

                    def down_part(t, gts):
                        a, n = TT[t]
                        w = 1 if t == 0 else 0
                        for dd in range(8):
                            qd = PS.get()
                            qdv = V(qd.t[:, 0:n], qd.r)
                            for i in range(2):
                                mm(qdv, pd.v(vd[:, i, dd * 128:(dd + 1) * 128]), V(gts[i].t[:, 0:n], gts[i].r), i == 0, i == 1)
                            stt(xv(dd, t), qdv, hg_col(j, dd, w), xv(dd, t), ALU.mult, ALU.add)
                            qd.done()
                        for gt in gts:
                            gt.done()

                    prev = None
                    done_tiles = []
                    for t in tiles:
                        gts = up_part(t)
                        if prev is not None:
                            down_part(*prev)
                            done_tiles.append(prev[0])
                            if tail is not None and g == 10 and len(done_tiles) >= 2:
                                tail(done_tiles[-2])
                        prev = (t, gts)
                    down_part(*prev)
                    done_tiles.append(prev[0])
                    if tail is not None and g == 10:
                        if len(done_tiles) >= 2:
                            tail(done_tiles[-2])
                        tail(done_tiles[-1])
                    w_free(pa)
                    w_free(pb)
                    w_free(pd)
                    if interleave is not None:
                        interleave(g)

            def rope(out, x, xP, g, gP, tab0, p0, p1, t, final=None):
                r0 = 8 * (t - 1)
                P = p1 - p0
                CR = V(TAB[p0:p1, tab0 + r0:tab0 + r0 + 8].unsqueeze(2).broadcast_to([P, 8, 64]), TABr)
                CC = V(TAB[p0:p1, tab0 + 32:tab0 + 96].unsqueeze(1).broadcast_to([P, 8, 64]), TABr)
                SR = V(TAB[p0:p1, tab0 + 96 + r0:tab0 + 96 + r0 + 8].unsqueeze(2).broadcast_to([P, 8, 64]), TABr)
                SC = V(TAB[p0:p1, tab0 + 128:tab0 + 192].unsqueeze(1).broadcast_to([P, 8, 64]), TABr)

                def v3(vv):
                    return V(vv.ap.rearrange("p (r c) -> p r c", c=64), vv.res)
                fa, fb = FP.get(), FP.get()
                a = V(fa.t[p0:p1, :], fa.r)
                b = V(fb.t[p0:p1, :], fb.r)
                stt(v3(a), v3(x), g, CR, ALU.mult, ALU.mult)
                tt_(v3(a), v3(a), CC, ALU.mult)
                stt(v3(b), v3(xP), gP, SR, ALU.mult, ALU.mult)
                tt_(v3(b), v3(b), SC, ALU.mult)
                if final is None:
                    for (o_, q0, q1) in out:
                        tt_(o_, V(fa.t[q0:q1, :], fa.r), V(fb.t[q0:q1, :], fb.r), ALU.add)
                else:
                    tt_(a, a, b, ALU.add)
                    for (o_, q0, q1) in out:
                        tt_(o_, V(fa.t[q0:q1, :], fa.r), V(final[0][q0:q1, 0:512], final[1]), ALU.mult)
                fa.done()
                fb.done()

            def mixer(l, last, ffn2_follows=True):
                lam_init = 0.8 - 0.6 * math.exp(-0.3 * l)

                def chk(k):
                    if stop_after == (l, f'dbg{k}'):
                        raise _Stop()
                qtiles = [1, 2, 3, 4] if last else [0, 1, 2, 3, 4]
                alltiles = [0, 1, 2, 3, 4]
                win = w_in_d[l].rearrange("(kc p) n -> p kc n", p=128)
                WS.cur_cap = 5

                prod = misc(0, 0, 2, 0, 64)
                tt_(V(MISC[0:64, 0:1], MISCr[0]), smcol(l, 124, 0, 64), smcol(l, 125, 0, 64), ALU.mult)
                tt_(V(MISC[0:64, 1:2], MISCr[0]), smcol(l, 126, 0, 64), smcol(l, 127, 0, 64), ALU.mult)
                pl = PS.get()
                mm(V(pl.t[:, 0:2], pl.r), V(ONESF[0:64, 0:128], ONESFr), prod, True, True)
                act(misc(0, 2, 2), V(pl.t[:, 0:2], pl.r), AF.Exp)
                pl.done()
                stt(misc(1, 4), misc(0, 3), -lam_init, misc(0, 2), ALU.add, ALU.subtract)
                ts(misc(1, 5), smcol(l, 111), 1.0 - lam_init, 0.0, ALU.mult, ALU.add)
                neglam = misc(1, 4)
                gsub = misc(1, 5)

                cq_s, ckv_s = aalloc(3), aalloc(2)
                kt_ss = aalloc(2)
                pq0 = w_get([(0, 8, 256, win[:, :, 0:256])])
                pq1 = w_get([(0, 8, 128, win[:, :, 256:384])])
                pkv = w_get([(0, 8, 256, win[:, :, 384:640])])
                pkr = w_get([(0, 8, 96, win[:, :, 576:672]),
                             (768, 8, 96, w_krP_d[l].rearrange("(kc p) n -> p kc n", p=128))])
                vq0, vq1, vkv = pq0.view(8, 256), pq1.view(8, 128), pkv.view(8, 256)
                vkr, vkrP = pkr.view(8, 96), pkr.view(8, 96, 768)

                def lat_norm(lhs_of, nchunks, dst_slots, gcol0, Dn, t):
                    a, n = TT[t]
                    banks = []
                    for cc_ in range(nchunks):
                        pb_ = PS.get()
                        bv = V(pb_.t[:, 0:n], pb_.r)
                        for kc in range(8):
                            mm(bv, lhs_of(cc_, kc), hv(kc, t), kc == 0, kc == 7)
                        banks.append(pb_)
                    pss = PS.get()
                    pv = V(pss.t[:, 0:n], pss.r)
                    for cc_ in range(nchunks):
                        sq = BP.get()
                        sv = V(sq.t[:, 0:n], sq.r)
                        act(sv, V(banks[cc_].t[:, 0:n], banks[cc_].r), AF.Square)
                        mm(pv, V(ONES[:, 0:128], ONESr), sv, cc_ == 0, cc_ == nchunks - 1)
                        sq.done()
                    rs = rstd_from(pv, n, 128, 1.0 / Dn, 0)
                    pss.done()
                    for cc_ in range(nchunks):
                        stt(slot_v(dst_slots[cc_], t), V(banks[cc_].t[:, 0:n], banks[cc_].r), smcol(l, gcol0 + cc_),
                            V(rs.t[:, 0:n], rs.r), ALU.mult, ALU.mult)
                        banks[cc_].done()
                    rs.done()

                for t in alltiles:
                    lat_norm(lambda cc_, kc: (pq0.v(vq0[:, kc, cc_ * 128:(cc_ + 1) * 128]) if cc_ < 2 else pq1.v(vq1[:, kc, 0:128])),
                             3, cq_s, 96, 384, t)
                for t in alltiles:
                    lat_norm(lambda cc_, kc: pkv.v(vkv[:, kc, cc_ * 128:(cc_ + 1) * 128]), 2, ckv_s, 99, 256, t)
                for kt_s in kt_ss:
                    S.op("pool", lambda kt_s=kt_s: nc.gpsimd.memset(AR[96:128, kt_s, :], 0.0), [], [slot_all(kt_s)])
                pssr = PS.get()
                for t in alltiles:
                    a, n = TT[t]
                    pr, pp = PS.get(), PS.get()
                    prv, ppv = V(pr.t[0:96, 0:n], pr.r), V(pp.t[0:96, 0:n], pp.r)
                    for kc in range(8):
                        mm(prv, pkr.v(vkr[:, kc, :]), hv(kc, t), kc == 0, kc == 7)
                    if t > 0:
                        for kc in range(8):
                            mm(ppv, pkr.v(vkrP[:, kc, :]), hv(kc, t), kc == 0, kc == 7)
                    sq = BP.get()
                    act(V(sq.t[0:96, 0:n], sq.r), prv, AF.Square)
                    for sub in range(n // 128):
                        kt = a // 128 + sub
                        mm(V(pssr.t[:, kt:kt + 1], pssr.r), V(sq.t[0:96, sub * 128:(sub + 1) * 128], sq.r),
                           V(MASKR[0:96, 0:1], CONSTr), True, True)
                    sq.done()
                    r64 = V(pr.t[64:96, 0:n], pr.r)
                    if t == 0:
                        for kt_s in kt_ss:
                            act(slot_v(kt_s, t, 64, 96), r64, AF.Identity, scale=smcol(l, 104, 64, 96))
                    else:
                        rope([(slot_v(kt_s, t, 64, 96), 64, 96) for kt_s in kt_ss], r64, V(pp.t[64:96, 0:n], pp.r),
                             smcol(l, 104, 64, 96), smcol(l, 106, 64, 96), 0, 64, 96, t)
                    pr.done()
                    pp.done()
                ssr = misc(2, 8, 18)
                vcopy(ssr, V(pssr.t[:, 0:18], pssr.r))
                pssr.done()
                for p_ in (pq0, pq1, pkv, pkr):
                    w_free(p_)
                if stop_after == (l, 'm1'):
                    raise _Stop()
                afree(Hs)

                ymla = aalloc(4)
                va_s = aalloc(2)
                vaug = [AR[:, va_s[i], :].rearrange("p (k e) -> p k e", e=128) for i in range(2)]
                S.op("pool", lambda: nc.gpsimd.memset(vaug[0][:, :, 64:128], 1.0), [], [slot_all(va_s[0])])
                S.op("pool", lambda: nc.gpsimd.memset(vaug[1][:, :, 0:64], 1.0), [], [slot_all(va_s[1])])
                WS.cur_cap = 8
                uq = w_uq_d[l].rearrange("(kc p) n -> p kc n", p=128)
                uqP = w_uqP_d[l].rearrange("(kc p) n -> p kc n", p=128)
                ukv = w_ukv_d[l].rearrange("(kc p) n -> p kc n", p=128)
                qcount = 0
                LAG = 3
                mla_pieces = {}

                def mla_setup(h):
                    par = h % 2
                    kt_s = kt_ss[par]
                    wq = w_get([(0, 3, 96, uq[:, :, h * 96:(h + 1) * 96]), (288, 3, 96, uqP[:, :, h * 96:(h + 1) * 96])])
                    wkv = w_get([(0, 2, 128, ukv[:, :, h * 128:(h + 1) * 128])])
                    mla_pieces[h] = (wq, wkv)
                    vwkv = wkv.view(2, 128)
                    pssk = PS.get()
                    for t in alltiles:
                        a, n = TT[t]
                        pk = PS.get()
                        pkv_ = V(pk.t[0:64, 0:n], pk.r)
                        for kc in range(2):
                            mm(pkv_, wkv.v(vwkv[:, kc, 0:64]), slot_v(ckv_s[kc], t), kc == 0, kc == 1)
                        sq = BP.get()
                        act(V(sq.t[0:64, 0:n], sq.r), pkv_, AF.Square)
                        for sub in range(n // 128):
                            kt = a // 128 + sub
                            mm(V(pssk.t[:, kt:kt + 1], pssk.r), V(sq.t[0:64, sub * 128:(sub + 1) * 128], sq.r),
                               V(ONES[0:64, 0:1], ONESr), True, True)
                        sq.done()
                        act(slot_v(kt_s, t, 0, 64), pkv_, AF.Identity, scale=smcol(l, 104, 0, 64))
                        pk.done()
                    rk = misc(3 + par, 32 + 18 * par, 18)
                    tt_(rk, V(pssk.t[:, 0:18], pssk.r), ssr, ALU.add)
                    pssk.done()
                    act(rk, rk, AF.Ln, bias=V(EPSC[:, 1:2], EPSCr))
                    act(rk, rk, AF.Exp, scale=-0.5)
                    off = 0 if par == 0 else 64
                    for k0 in range(0, 18, 8):
                        nk = min(8, 18 - k0)
                        pvb = PS.get()
                        pv3 = pvb.t[:, 0:nk * 64].rearrange("p (k e) -> p k e", e=64)
                        for i in range(nk):
                            kt = k0 + i
                            for kc in range(2):
                                mm(V(pv3[:, i, :], pvb.r), slot_cols(ckv_s[kc], kt * 128, 128), wkv.v(vwkv[:, kc, 64:128]),
                                   kc == 0, kc == 1)
                        res = [ARres[va_s[par]][t] for t, (a, m) in enumerate(TT) if a < (k0 + nk) * 128 and k0 * 128 < a + m]
                        vcopy(V(vaug[par][:, k0:k0 + nk, off:off + 64], res), V(pv3, pvb.r))
                        pvb.done()

                def mla_prep(h, t, qi):
                    wq = mla_pieces[h][0]
                    vwq, vwqP = wq.view(3, 96), wq.view(3, 96, 288)
                    a, n = TT[t]
                    pn = PS.get()
                    pnv = V(pn.t[0:96, 0:n], pn.r)
                    for kc in range(3):
                        mm(pnv, wq.v(vwq[:, kc, :]), slot_v(cq_s[kc], t), kc == 0, kc == 2)
                    if t > 0:
                        pp = PS.get()
                        ppv = V(pp.t[0:96, 0:n], pp.r)
                        for kc in range(3):
                            mm(ppv, wq.v(vwqP[:, kc, :]), slot_v(cq_s[kc], t), kc == 0, kc == 2)
                    sqn = BP.get()
                    act(V(sqn.t[0:96, 0:n], sqn.r), pnv, AF.Square)
                    pss = PS.get()
                    pssv = V(pss.t[0:96, 0:n], pss.r)
                    mm(pssv, V(ONES[0:96, 0:96], ONESr), V(sqn.t[0:96, 0:n], sqn.r), True, True)
                    sqn.done()
                    rs = rstd_from(pssv, n, 96, 1.0 / 96, 0)
                    pss.done()
                    qt_, qr_ = QM[qi % 2]
                    if t == 0:
                        stt(V(qt_[0:96, 0:n], qr_), pnv, smcol(l, 101, 0, 96), V(rs.t[0:96, 0:n], rs.r), ALU.mult, ALU.mult)
                    else:
                        stt(V(qt_[0:64, 0:n], qr_), V(pn.t[0:64, 0:n], pn.r), smcol(l, 101, 0, 64), V(rs.t[0:64, 0:n], rs.r), ALU.mult, ALU.mult)
                        rope([(V(qt_[64:96, 0:n], qr_), 64, 96)], V(pn.t[64:96, 0:n], pn.r), V(pp.t[64:96, 0:n], pp.r),
                             smcol(l, 101, 64, 96), smcol(l, 103, 64, 96), 0, 64, 96, t, final=(rs.t, rs.r))
                        pp.done()
                    pn.done()
                    rs.done()
                    return V(qt_[:, 0:n], qr_)

                def mla_attend(h, t, qv):
                    par = h % 2
                    kt_s = kt_ss[par]
                    a, n = TT[t]
                    acc = PS.get()
                    accv = V(acc.t[:, 0:n], acc.r)
                    kts = list(range(18)) if t > 0 else [0, 1]
                    pend = []
                    for i in range(len(kts) + LAG):
                        if i < len(kts):
                            kt = kts[i]
                            sps = PS.get()
                            sv = V(sps.t[:, 0:n], sps.r)
                            mm(sv, slot_cols(kt_s, kt * 128, 128), qv, True, True)
                            pt = BP.get()
                            act(V(pt.t[:, 0:n], pt.r), sv, AF.Exp, scale=V(MISC[:, 32 + 18 * par + kt:32 + 18 * par + kt + 1], MISCr[3 + par]))
                            sps.done()
                            pend.append((kt, pt))
                        if i >= LAG:
                            kt, pt = pend.pop(0)
                            res = [ARres[va_s[par]][tt2] for tt2, (a2, m2) in enumerate(TT) if a2 <= kt * 128 < a2 + m2]
                            mm(accv, V(vaug[par][:, kt, :], res), V(pt.t[:, 0:n], pt.r), kt == kts[0], kt == kts[-1])
                            pt.done()
                    rc = FP.get()
                    if par == 0:
                        recip(V(rc.t[64:128, 0:n], rc.r), V(acc.t[64:128, 0:n], acc.r))
                        tt_(slot_v(ymla[h // 2], t, 0, 64), V(acc.t[0:64, 0:n], acc.r), V(rc.t[64:128, 0:n], rc.r), ALU.mult)
                    else:
                        recip(V(rc.t[0:64, 0:n], rc.r), V(acc.t[0:64, 0:n], acc.r))
                        tt_(slot_v(ymla[h // 2], t, 64, 128), V(acc.t[64:128, 0:n], acc.r), V(rc.t[0:64, 0:n], rc.r), ALU.mult)
                    rc.done()
                    acc.done()

                units = [(h, t) for h in range(8) for t in qtiles]
                mla_setup(0)
                qcur = mla_prep(units[0][0], units[0][1], 0)
                for ui, (h, t) in enumerate(units):
                    qnext = None
                    if ui + 1 < len(units):
                        h2, t2 = units[ui + 1]
                        if h2 != h:
                            mla_setup(h2)
                        qnext = mla_prep(h2, t2, ui + 1)
                    mla_attend(h, t, qcur)
                    if ui + 1 == len(units) or units[ui + 1][0] != h:
                        w_free(mla_pieces[h][0])
                        w_free(mla_pieces[h][1])
                    qcur = qnext
                if stop_after == (l, 'mla'):
                    raise _Stop()
                afree(kt_ss + va_s + cq_s + ckv_s)
                WS.cur_cap = 3

                Hs[:] = aalloc(8)
                compute_H(l, 1, alltiles)
                ydiff = aalloc(4)
                dk_s = aalloc(1)[0]
                dv_s = aalloc(1)[0]
                dvv = AR[:, dv_s, :].rearrange("p (k e) -> p k e", e=128)
                dqP = w_dqP_d[l].rearrange("(kc p) n -> p kc n", p=128)
                dkP = w_dkP_d[l].rearrange("(kc p) n -> p kc n", p=128)
                for h in range(4):
                    wdq = w_get([(0, 8, 128, win[:, :, 672 + 128 * h:672 + 128 * (h + 1)]), (1024, 8, 128, dqP[:, :, 128 * h:128 * (h + 1)])], cap=3)
                    wdk = w_get([(0, 8, 128, win[:, :, 1184 + 128 * h:1184 + 128 * (h + 1)]), (1024, 8, 128, dkP[:, :, 128 * h:128 * (h + 1)])], cap=3)
                    wdv = w_get([(0, 8, 128, win[:, :, 1696 + 128 * h:1696 + 128 * (h + 1)])], cap=3)
                    vdq, vdqP = wdq.view(8, 128), wdq.view(8, 128, 1024)
                    vdk, vdkP = wdk.view(8, 128), wdk.view(8, 128, 1024)
                    vdv = wdv.view(8, 128)
                    pssd = PS.get()
                    for t in alltiles:
                        a, n = TT[t]
                        pk = PS.get()
                        pkv_ = V(pk.t[:, 0:n], pk.r)
                        for kc in range(8):
                            mm(pkv_, wdk.v(vdk[:, kc, :]), hv(kc, t), kc == 0, kc == 7)
                        if t > 0:
                            pp = PS.get()
                            ppv = V(pp.t[:, 0:n], pp.r)
                            for kc in range(8):
                                mm(ppv, wdk.v(vdkP[:, kc, :]), hv(kc, t), kc == 0, kc == 7)
                        sq = BP.get()
                        act(V(sq.t[:, 0:n], sq.r), pkv_, AF.Square)
                        for sub in range(n // 128):
                            kt = a // 128 + sub
                            mm(V(pssd.t[:, 2 * kt:2 * kt + 2], pssd.r), V(sq.t[:, sub * 128:(sub + 1) * 128], sq.r),
                               V(BLK2[:, 0:2], CONSTr), True, True)
                        sq.done()
                        if t == 0:
                            act(slot_v(dk_s, t), pkv_, AF.Identity, scale=smcol(l, 109))
                        else:
                            rope([(slot_v(dk_s, t), 0, 128)], pkv_, ppv, smcol(l, 109), smcol(l, 110), 192, 0, 128, t)
                            pp.done()
                        pk.done()
                    rkd = misc(5 + (h % 2), 68 + 36 * (h % 2), 36)
                    act(rkd, V(pssd.t[:, 0:36], pssd.r), AF.Ln, bias=V(EPSC[:, 2:3], EPSCr))
                    pssd.done()
                    act(rkd, rkd, AF.Exp, scale=-0.5)
                    rkbase = 68 + 36 * (h % 2)
                    for k0 in range(0, 18, 4):
                        nk = min(4, 18 - k0)
                        pvb = PS.get()
                        pv3 = pvb.t[:, 0:nk * 128].rearrange("p (k e) -> p k e", e=128)
                        for i in range(nk):
                            kt = k0 + i
                            for kc in range(8):
                                mm(V(pv3[:, i, :], pvb.r), slot_cols(Hs[kc], kt * 128, 128), wdv.v(vdv[:, kc, :]), kc == 0, kc == 7)
                        res = [ARres[dv_s][t] for t, (a, m_) in enumerate(TT) if a < (k0 + nk) * 128 and k0 * 128 < a + m_]
                        vcopy(V(dvv[:, k0:k0 + nk, :], res), V(pv3, pvb.r))
                        pvb.done()
                    def diff_prep(t, qi):
                        a, n = TT[t]
                        pq = PS.get()
                        pqv = V(pq.t[:, 0:n], pq.r)
                        for kc in range(8):
                            mm(pqv, wdq.v(vdq[:, kc, :]), hv(kc, t), kc == 0, kc == 7)
                        if t > 0:
                            pp = PS.get()
                            ppv = V(pp.t[:, 0:n], pp.r)
                            for kc in range(8):
                                mm(ppv, wdq.v(vdqP[:, kc, :]), hv(kc, t), kc == 0, kc == 7)
                        sq = BP.get()
                        act(V(sq.t[:, 0:n], sq.r), pqv, AF.Square)
                        pss = PS.get()
                        pssv = V(pss.t[:, 0:n], pss.r)
                        mm(pssv, V(BLKONES[:, 0:128], CONSTr), V(sq.t[:, 0:n], sq.r), True, True)
                        sq.done()
                        rs = rstd_from(pssv, n, 128, 1.0 / 64, 0)
                        pss.done()
                        (qa_t, qa_r), (qb_t, qb_r) = QD[qi % 2]
                        if t == 0:
                            stt(V(qa_t[0:64, 0:n], qa_r), V(pq.t[0:64, 0:n], pq.r), smcol(l, 107, 0, 64), V(rs.t[0:64, 0:n], rs.r), ALU.mult, ALU.mult)
                            stt(V(qb_t[64:128, 0:n], qb_r), V(pq.t[64:128, 0:n], pq.r), smcol(l, 107, 64, 128), V(rs.t[64:128, 0:n], rs.r), ALU.mult, ALU.mult)
                        else:
                            rope([(V(qa_t[0:64, 0:n], qa_r), 0, 64), (V(qb_t[64:128, 0:n], qb_r), 64, 128)], pqv, ppv,
                                 smcol(l, 107), smcol(l, 108), 192, 0, 128, t, final=(rs.t, rs.r))
                            pp.done()
                        pq.done()
                        rs.done()
                        return (V(qa_t[:, 0:n], qa_r), V(qb_t[:, 0:n], qb_r))

                    def diff_attend(t, qpair):
                        a, n = TT[t]
                        kts = list(range(18)) if t > 0 else [0, 1]
                        ocomp = []
                        for m in range(2):
                            dqv = qpair[m]
                            accO, accR = PS.get(), PS.get()
                            aov, arv = V(accO.t[:, 0:n], accO.r), V(accR.t[:, 0:n], accR.r)
                            pend = []
                            for i in range(len(kts) + LAG):
                                if i < len(kts):
                                    kt = kts[i]
                                    sps = PS.get()
                                    sv = V(sps.t[:, 0:n], sps.r)
                                    mm(sv, slot_cols(dk_s, kt * 128, 128), dqv, True, True)
                                    pt = BP.get()
                                    act(V(pt.t[:, 0:n], pt.r), sv, AF.Exp,
                                        scale=V(MISC[:, rkbase + 2 * kt + m:rkbase + 2 * kt + m + 1], MISCr[5 + (h % 2)]))
                                    sps.done()
                                    pend.append((kt, pt))
                                if i >= LAG:
                                    kt, pt = pend.pop(0)
                                    res = [ARres[dv_s][tt2] for tt2, (a2, m2) in enumerate(TT) if a2 <= kt * 128 < a2 + m2]
                                    mm(aov, V(dvv[:, kt, :], res), V(pt.t[:, 0:n], pt.r), kt == kts[0], kt == kts[-1])
                                    mm(arv, V(ONES[:, 0:128], ONESr), V(pt.t[:, 0:n], pt.r), kt == kts[0], kt == kts[-1])
                                    pt.done()
                            rc = FP.get()
                            recip(V(rc.t[:, 0:n], rc.r), arv)
                            accR.done()
                            oc = FP.get()
                            tt_(V(oc.t[:, 0:n], oc.r), aov, V(rc.t[:, 0:n], rc.r), ALU.mult)
                            accO.done()
                            rc.done()
                            ocomp.append(oc)
                        o0, o1 = ocomp
                        ov = V(o0.t[:, 0:n], o0.r)
                        stt(ov, V(o1.t[:, 0:n], o1.r), neglam, ov, ALU.mult, ALU.add)
                        o1.done()
                        sq = BP.get()
                        act(V(sq.t[:, 0:n], sq.r), ov, AF.Square)
                        pss = PS.get()
                        pssv = V(pss.t[:, 0:n], pss.r)
                        mm(pssv, V(ONES[:, 0:128], ONESr), V(sq.t[:, 0:n], sq.r), True, True)
                        sq.done()
                        rs = rstd_from(pssv, n, 128, 1.0 / 128, 0)
                        pss.done()
                        stt(slot_v(ydiff[h], t), ov, gsub, V(rs.t[:, 0:n], rs.r), ALU.mult, ALU.mult)
                        rs.done()
                        o0.done()

                    qcur = diff_prep(qtiles[0], qcount)
                    for ti, t in enumerate(qtiles):
                        qcount += 1
                        qnext = diff_prep(qtiles[ti + 1], qcount) if ti + 1 < len(qtiles) else None
                        diff_attend(t, qcur)
                        qcur = qnext
                    w_free(wdq)
                    w_free(wdk)
                    w_free(wdv)
                if stop_after == (l, 'diff'):
                    raise _Stop()
                afree([dk_s, dv_s])

                mtiles = qtiles
                wbr = w_br_d[l]
                wo = w_o_d[l]

                def merge(branches, ys, tail=None):
                    Ms = aalloc(2)
                    for p_ in range(4):
                        for di in range(2):
                            dd = 2 * p_ + di
                            wg = w_get([(i * 1024, 8, 128, win[:, :, 3744 + br * 1024 + dd * 128:3744 + br * 1024 + (dd + 1) * 128])
                                        for i, br in enumerate(branches)], cap=3)
                            wb = w_get([(i * 512, 4, 128, wbr[br][:, dd * 128:(dd + 1) * 128].rearrange("(c p) n -> p c n", p=128))
                                        for i, br in enumerate(branches)], cap=3)
                            for t in mtiles:
                                a, n = TT[t]
                                terms = []
                                for i, br in enumerate(branches):
                                    vg, vb = wg.view(8, 128, i * 1024), wb.view(4, 128, i * 512)
                                    pg, pz = PS.get(), PS.get()
                                    pgv, pzv = V(pg.t[:, 0:n], pg.r), V(pz.t[:, 0:n], pz.r)
                                    for kc in range(8):
                                        mm(pgv, wg.v(vg[:, kc, :]), hv(kc, t), kc == 0, kc == 7)
                                    for c in range(4):
                                        mm(pzv, wb.v(vb[:, c, :]), slot_v(ys[i][c], t), c == 0, c == 3)
                                    sg = FP.get()
                                    sgv = V(sg.t[:, 0:n], sg.r)
                                    act(sgv, pgv, AF.Sigmoid)
                                    pg.done()
                                    if len(branches) == 1:
                                        tt_(slot_v(Ms[di], t), sgv, pzv, ALU.mult)
                                        sg.done()
                                    else:
                                        tt_(sgv, sgv, pzv, ALU.mult)
                                        terms.append(sg)
                                    pz.done()
                                if len(branches) == 2:
                                    tt_(slot_v(Ms[di], t), V(terms[0].t[:, 0:n], terms[0].r), V(terms[1].t[:, 0:n], terms[1].r), ALU.add)
                                    terms[0].done()
                                    terms[1].done()
                            w_free(wg)
                            w_free(wb)
                        wop = w_get([(0, 2, 1024, wo[2 * p_ * 128:(2 * p_ + 2) * 128, :].rearrange("(c p) n -> p c n", p=128))], cap=3)
                        vo = wop.view(2, 1024)
                        for t in mtiles:
                            a, n = TT[t]
                            w = 1 if t == 0 else 0
                            for do in range(8):
                                po = PS.get()
                                pov = V(po.t[:, 0:n], po.r)
                                for di in range(2):
                                    mm(pov, wop.v(vo[:, di, do * 128:(do + 1) * 128]), slot_v(Ms[di], t), di == 0, di == 1)
                                stt(xv(do, t), pov, modcol(l, 5, do, w), xv(do, t), ALU.mult, ALU.add)
                                po.done()
                            if tail is not None and p_ == 3:
                                ti = mtiles.index(t)
                                if ti > 0:
                                    tail(mtiles[ti - 1])
                        if tail is not None and p_ == 3:
                            tail(mtiles[-1])
                        w_free(wop)
                    afree(Ms)

                merge([0, 1], [ymla, ydiff])
                if stop_after == (l, 'merge1'):
                    raise _Stop()
                afree(ymla + ydiff)

                yconv = aalloc(4)
                u_s = aalloc(1)[0]
                for c in range(4):
                    wcx = w_get([(0, 8, 128, win[:, :, 2720 + c * 128:2720 + (c + 1) * 128]),
                                 (1024, 8, 128, win[:, :, 3232 + c * 128:3232 + (c + 1) * 128])], cap=3)
                    wcb = w_get([(0, 8, 128, win[:, :, 2208 + c * 128:2208 + (c + 1) * 128])], cap=3)
                    vcc, vcx, vcb = wcx.view(8, 128), wcx.view(8, 128, 1024), wcb.view(8, 128)
                    for t in alltiles:
                        a, n = TT[t]
                        pc, px = PS.get(), PS.get()
                        pcv, pxv = V(pc.t[:, 0:n], pc.r), V(px.t[:, 0:n], px.r)
                        for kc in range(8):
                            mm(pcv, wcx.v(vcc[:, kc, :]), hv(kc, t), kc == 0, kc == 7)
                        for kc in range(8):
                            mm(pxv, wcx.v(vcx[:, kc, :]), hv(kc, t), kc == 0, kc == 7)
                        sc = FP.get()
                        act(V(sc.t[:, 0:n], sc.r), pcv, AF.Copy)
                        pc.done()
                        tt_(slot_v(u_s, t), pxv, V(sc.t[:, 0:n], sc.r), ALU.mult)
                        px.done()
                        sc.done()
                    for t in mtiles:
                        a, n = TT[t]
                        seg_s, seg_e = (0, NCTX) if t == 0 else (NCTX, NTOK)
                        ac = FP.get()
                        acv = V(ac.t[:, 0:n], ac.r)
                        cw = lambda j: smcol(l, 112 + c * 3 + j)
                        ts(acv, slot_v(u_s, t), cw(1), 0.0, ALU.mult, ALU.add)
                        lo = max(a, seg_s + 1)
                        stt(V(ac.t[:, lo - a:n], ac.r), slot_cols(u_s, lo - 1, a + n - lo), cw(0), V(ac.t[:, lo - a:n], ac.r), ALU.mult, ALU.add)
                        hi = min(a + n, seg_e - 1)
                        stt(V(ac.t[:, 0:hi - a], ac.r), slot_cols(u_s, a + 1, hi - a), cw(2), V(ac.t[:, 0:hi - a], ac.r), ALU.mult, ALU.add)
                        pb_ = PS.get()
                        pbv = V(pb_.t[:, 0:n], pb_.r)
                        for kc in range(8):
                            mm(pbv, wcb.v(vcb[:, kc, :]), hv(kc, t), kc == 0, kc == 7)
                        tt_(slot_v(yconv[c], t), pbv, acv, ALU.mult)
                        pb_.done()
                        ac.done()
                    w_free(wcx)
                    w_free(wcb)
                afree([u_s])
                merge([2], [yconv], tail=(lambda t_: compute_H(l, 2, [t_])) if ffn2_follows else None)
                afree(yconv)
                WS.cur_cap = 12

            def network():
                mod_begin(0)
                mod_pieces(0, 12)
                mod_finish(0, 0, 24, release=False)
                rest_done = [False]

                def il0(g):
                    mod_pieces(0, 4)
                    if mod_state[0][1] == 36 and not rest_done[0]:
                        mod_finish(0, 24, 72)
                        derive(0, js=(1, 2))
                        rest_done[0] = True

                for l in range(n_layers):
                    last = (l == DEPTH - 1)
                    derive(l, js=(0,) if l == 0 else (0, 1, 2))
                    WS.cur_cap = 12
                    stop1 = stop_after == (l, "ffn1")
                    ffn(l, 0, f1u_d, f1d_d, [0, 1, 2, 3, 4], interleave=il0 if l == 0 else None,
                        tail=None if stop1 else (lambda t_, l=l: compute_H(l, 1, [t_])))
                    if stop1:
                        break
                    mixer(l, last, ffn2_follows=(stop_after != (l, "mix")))
                    if stop_after == (l, "mix"):
                        break
                    nxt = l + 1 < n_layers
                    if nxt:
                        mod_begin(l + 1)
                    ffn(l, 2, f2u_d, f2d_d, [1, 2, 3, 4] if last else [0, 1, 2, 3, 4], h_ready=True,
                        interleave=(lambda g, l=l: mod_pieces(l + 1, 4)) if nxt else None)
                    if nxt:
                        mod_pieces(l + 1, 36)
                        mod_finish(l + 1)

            try:
                network()
            except _Stop:
                pass
            outr = Res("out")
            for c in range(8):
                S.dma("sp", [(outT_d[c * 128:(c + 1) * 128, :], XT[:, c, NCTX:NTOK])], outr, reads=[V(None, Xres[c][1:5])])
                if debug_ctx:
                    S.dma("sp", [(outC_d[c * 128:(c + 1) * 128, :], XT[:, c, 0:NCTX])], outr, reads=[V(None, Xres[c][0])])
            S.wait_all("sp", [V(None, [Xres[c][t] for c in range(8) for t in range(5)])])
            if stop_after is None or stop_after[1] in ("ffn1", "mix"):
                afree(Hs)

        S.dry = True
        program()
        if stop_after is None:
            assert len(arena_free) == NSLOT and not PS.live and not FP.live and not BP.live, (len(arena_free), PS.live, FP.live, BP.live)
        arena_free[:] = list(range(NSLOT))
        PS.free[:] = list(range(8))
        FP.free[:] = list(range(len(FP.tiles)))
        BP.free[:] = list(range(len(BP.tiles)))
        S.dry = False
        program()
        if stop_after is None:
            assert WS.nxt == len(WS.descs) and WS.outstanding == 0
        build_program.stats = dict(ninst=S.ninst, nwait=S.nwait, npieces=len(WS.descs),
                                   nsig={k: e.nsig for k, e in S.E.items()})
    return nc


def _perm(R):
    q = R // 4
    idx = np.arange(R).reshape(2, 2, q)
    return idx[:, ::-1, :].reshape(R)


def _tables():
    tab = np.zeros((128, NTAB), np.float32)

    def fill(base, R, rows0):
        q = R // 4
        invf = (10000.0 ** (-np.arange(q, dtype=np.float32) / q)).astype(np.float32)
        rows = np.arange(32, dtype=np.float32)
        cols = np.arange(64, dtype=np.float32)
        for d in range(R):
            a, h, f = d // (2 * q), (d // q) % 2, d % q
            sign = -1.0 if h == 0 else 1.0
            for r0_ in rows0:
                p = r0_ + d
                if a == 0:
                    ang = (rows * invf[f]).astype(np.float32)
                    tab[p, base:base + 32] = np.cos(ang)
                    tab[p, base + 32:base + 96] = 1.0
                    tab[p, base + 96:base + 128] = sign * np.sin(ang)
                    tab[p, base + 128:base + 192] = 1.0
                else:
                    ang = (cols * invf[f]).astype(np.float32)
                    tab[p, base:base + 32] = 1.0
                    tab[p, base + 32:base + 96] = np.cos(ang)
                    tab[p, base + 96:base + 128] = 1.0
                    tab[p, base + 128:base + 192] = sign * np.sin(ang)
    fill(0, 32, [64])
    fill(192, 64, [0, 64])
    return tab


def _host_prep(inputs):
    f = lambda k: np.ascontiguousarray(np.asarray(inputs[k], dtype=np.float32))
    P32, P64 = _perm(32), _perm(64)
    w_in = f("w_in")
    w_uq = f("w_uq")
    sm = np.zeros((128, DEPTH, NS), np.float32)
    b_mod, norm_g = f("b_mod"), f("norm_g")
    g_cq, g_ckv = f("g_cq"), f("g_ckv")
    gq, gk, gqd, gkd = f("g_q_mla"), f("g_k_mla"), f("g_q_diff"), f("g_k_diff")
    gsub, convw, lam = f("g_subln"), f("conv_w"), f("lam")
    for l in range(DEPTH):
        sm[:, l, 0:72] = b_mod[l].reshape(72, 128).T
        sm[:, l, 72:96] = norm_g[l].reshape(24, 128).T
        sm[:, l, 96:99] = g_cq[l].reshape(3, 128).T
        sm[:, l, 99:101] = g_ckv[l].reshape(2, 128).T
        sm[0:96, l, 101] = gq[l, 0:96]
        sm[64:96, l, 103] = gq[l, 64:96][P32]
        sm[0:96, l, 104] = gk[l, 0:96]
        sm[64:96, l, 106] = gk[l, 64:96][P32]
        for r0_ in (0, 64):
            sm[r0_:r0_ + 64, l, 107] = gqd[l]
            sm[r0_:r0_ + 64, l, 108] = gqd[l][P64]
            sm[r0_:r0_ + 64, l, 109] = gkd[l]
            sm[r0_:r0_ + 64, l, 110] = gkd[l][P64]
        sm[:, l, 111] = gsub[l]
        sm[:, l, 112:124] = convw[l].reshape(3, 4, 128).transpose(2, 1, 0).reshape(128, 12)
        sm[0:64, l, 124:128] = lam[l].T
    shared = {
        "smalls": np.ascontiguousarray(sm.reshape(128, DEPTH * NS)),
        "tabs": _tables(),
        "w_mod": f("w_mod"), "ffn1_up": f("ffn1_up"), "ffn1_down": f("ffn1_down"),
        "ffn2_up": f("ffn2_up"), "ffn2_down": f("ffn2_down"), "w_in": w_in,
        "w_krP": np.ascontiguousarray(np.concatenate([w_in[:, :, 576:640], w_in[:, :, 640:672][:, :, P32]], axis=2)),
        "w_dqP": np.ascontiguousarray(w_in[:, :, 672:1184].reshape(DEPTH, D, 8, 64)[:, :, :, P64].reshape(DEPTH, D, 512)),
        "w_dkP": np.ascontiguousarray(w_in[:, :, 1184:1696].reshape(DEPTH, D, 8, 64)[:, :, :, P64].reshape(DEPTH, D, 512)),
        "w_uq": w_uq,
        "w_uqP": np.ascontiguousarray(np.concatenate([w_uq.reshape(DEPTH, 384, 8, 96)[:, :, :, 0:64],
                                                      w_uq.reshape(DEPTH, 384, 8, 96)[:, :, :, 64:96][:, :, :, P32]], axis=3).reshape(DEPTH, 384, 768)),
        "w_ukv": f("w_ukv"), "w_br": f("w_br"), "w_o": f("w_o"),
    }
    x, c, ctx, c_ctx = f("x"), f("c"), f("ctx"), f("c_ctx")
    in_maps = []
    for b in range(8):
        cc = np.zeros((128, 8, 2), np.float32)
        cc[:, :, 0] = c[b].reshape(8, 128).T
        cc[:, :, 1] = c_ctx.reshape(8, 128).T
        m = dict(shared)
        m["xT"] = np.ascontiguousarray(x[b].T)
        m["ctxT"] = np.ascontiguousarray(ctx[b].T)
        m["cc"] = np.ascontiguousarray(cc.reshape(128, 16))
        in_maps.append(m)
    return in_maps


_NC_CACHE = {}


def kernel(**inputs):
    in_maps = _host_prep(inputs)
    if "nc" not in _NC_CACHE:
        _NC_CACHE["nc"] = build_program()
    nc = _NC_CACHE["nc"]
    res = run_bass_kernel_spmd(nc, in_maps, core_ids=list(range(8)))
    out = np.stack([np.asarray(res.results[b]["outT"]).T for b in range(8)], axis=0)
    return np.ascontiguousarray(out.astype(np.float32))
```
